# Optimizing a Trainium2 kernel written in Bass

```python
import math
import jax, jax.numpy as jnp
from jax import lax
import numpy as np

D_MODEL = 1024
BATCH = 4
SEQ = 8192
DEPTH = 2

N_A_LAYERS = DEPTH // 2
N_B_LAYERS = DEPTH - N_A_LAYERS

EXPAND = 2
A_WIDTH = EXPAND * D_MODEL
A_GROUPS = 8
A_GROUP_DIM = A_WIDTH // A_GROUPS
CHUNK = 128

HEAD_DIM = 64
N_Q_HEADS = D_MODEL // HEAD_DIM
N_KV_HEADS = max(1, N_Q_HEADS // 8)
Q_PER_KV = N_Q_HEADS // N_KV_HEADS
B_WIDTH = N_Q_HEADS * HEAD_DIM
KV_WIDTH = N_KV_HEADS * HEAD_DIM
WINDOW = 128

REL_BUCKETS = 32
REL_MAX_DIST = 128

ALPHA = (2.0 * DEPTH) ** 0.25
BETA = (8.0 * DEPTH) ** -0.25
LN_EPS = 1e-5
NEG_INF = -1e30

kernel_name = "yoco_gmlp_swa_sink_hybrid"


def layer_norm(x, g, b):
    xf = x.astype(jnp.float32)
    mu = jnp.mean(xf, axis=-1, keepdims=True)
    var = jnp.mean(jnp.square(xf - mu), axis=-1, keepdims=True)
    y = (xf - mu) * lax.rsqrt(var + LN_EPS)
    return (y * g.astype(jnp.float32) + b.astype(jnp.float32)).astype(x.dtype)


def rel_bucket(d):
    max_exact = REL_BUCKETS // 2
    df = jnp.maximum(d, 1).astype(jnp.float32)
    large = max_exact + (jnp.log(df / max_exact) / math.log(REL_MAX_DIST / max_exact)
                         * (REL_BUCKETS - max_exact)).astype(jnp.int32)
    large = jnp.minimum(large, REL_BUCKETS - 1)
    return jnp.where(d < max_exact, d, large)


def sgu_branch(h, w_in, ln_g, ln_b, w_spatial, b_spatial, w_out):
    bsz, seq, _ = h.shape
    nc = seq // CHUNK
    u, v, z = jnp.split(h @ w_in, 3, axis=-1)
    v = layer_norm(v, ln_g, ln_b).reshape(bsz, nc, CHUNK, A_GROUPS, A_GROUP_DIM)
    tri = jnp.tril(jnp.ones((CHUNK, CHUNK), dtype=bool))
    ws = jnp.where(tri, w_spatial, jnp.zeros((), w_spatial.dtype))
    s = jnp.einsum("gts,bcsgd->bctgd", ws, v) + b_spatial.T[:, :, None]
    y = u * s.reshape(bsz, seq, A_WIDTH) * jax.nn.silu(z)
    return y @ w_out


def shared_kv_bands(h, w_kv):
    bsz, seq, _ = h.shape
    nc = seq // CHUNK
    k, v = jnp.split(h @ w_kv, 2, axis=-1)

    def band(t):
        t = t.reshape(bsz, seq, N_KV_HEADS, HEAD_DIM)
        prev = jnp.pad(t, ((0, 0), (CHUNK, 0), (0, 0), (0, 0)))[:, :seq]
        prev = prev.reshape(bsz, nc, CHUNK, N_KV_HEADS, HEAD_DIM)
        cur = t.reshape(bsz, nc, CHUNK, N_KV_HEADS, HEAD_DIM)
        return jnp.concatenate([prev, cur], axis=2)

    return band(k), band(v)


def band_bias_and_mask(rel_bias, nc):
    t = jnp.arange(CHUNK, dtype=jnp.int32)[:, None]
    j = jnp.arange(2 * CHUNK, dtype=jnp.int32)[None, :]
    d = t + CHUNK - j
    in_window = (d >= 0) & (d < WINDOW)
    bias = rel_bias[rel_bucket(jnp.clip(d, 0, REL_MAX_DIST - 1))]
    bias = jnp.transpose(bias, (2, 0, 1)).astype(jnp.float32)
    bias = bias.reshape(N_KV_HEADS, Q_PER_KV, 1, CHUNK, 2 * CHUNK)
    has_prev = (jnp.arange(nc)[:, None, None] > 0) | (j[None] >= CHUNK)
    mask = in_window[None] & has_prev
    return bias, mask


def swa_branch(h, k_band, v_band, bias, mask, w_in, sinks, w_out):
    bsz, seq, _ = h.shape
    nc = seq // CHUNK
    q, z = jnp.split(h @ w_in, 2, axis=-1)
    q = q.reshape(bsz, nc, CHUNK, N_KV_HEADS, Q_PER_KV, HEAD_DIM)
    scores = jnp.einsum("bcqkgd,bcjkd->bkgcqj", q, k_band)
    logits = jnp.where(mask, scores.astype(jnp.float32) * (HEAD_DIM ** -0.5) + bias, NEG_INF)
    sink = sinks.astype(jnp.float32).reshape(N_KV_HEADS, Q_PER_KV, 1, 1, 1)
    m = jnp.maximum(jnp.max(logits, axis=-1, keepdims=True), sink)
    e = jnp.exp(logits - m)
    p = e / (jnp.sum(e, axis=-1, keepdims=True) + jnp.exp(sink - m))
    o = jnp.einsum("bkgcqj,bcjkd->bcqkgd", p.astype(v_band.dtype), v_band)
    y = o.reshape(bsz, seq, B_WIDTH) * jax.nn.silu(z)
    return y @ w_out


def setup_inputs(seed: int = 0) -> dict:
    key = jax.random.key(seed)
    ks = jax.random.split(key, 16)
    f32 = jnp.float32
    nrm = lambda k, shape, s: jax.random.normal(k, shape, f32) * s
    return {
        "x": nrm(ks[0], (BATCH, SEQ, D_MODEL), 1.0),
        "w_in_a": nrm(ks[1], (N_A_LAYERS, D_MODEL, 3 * A_WIDTH), D_MODEL ** -0.5),
        "sgu_ln_g": 1.0 + nrm(ks[2], (N_A_LAYERS, A_WIDTH), 0.1),
        "sgu_ln_b": nrm(ks[3], (N_A_LAYERS, A_WIDTH), 0.1),
        "w_spatial": nrm(ks[4], (N_A_LAYERS, A_GROUPS, CHUNK, CHUNK), 0.1),
        "b_spatial": 1.0 + nrm(ks[5], (N_A_LAYERS, A_GROUPS, CHUNK), 0.1),
        "w_out_a": nrm(ks[6], (N_A_LAYERS, A_WIDTH, D_MODEL), BETA * A_WIDTH ** -0.5),
        "w_kv": nrm(ks[7], (D_MODEL, 2 * KV_WIDTH), D_MODEL ** -0.5),
        "w_in_b": nrm(ks[8], (N_B_LAYERS, D_MODEL, 2 * B_WIDTH), D_MODEL ** -0.5),
        "attn_sinks": nrm(ks[9], (N_B_LAYERS, N_Q_HEADS), 0.5),
        "rel_bias": nrm(ks[10], (REL_BUCKETS, N_Q_HEADS), 0.5),
        "w_out_b": nrm(ks[11], (N_B_LAYERS, B_WIDTH, D_MODEL), BETA * B_WIDTH ** -0.5),
        "post_ln_g": 1.0 + nrm(ks[12], (DEPTH, D_MODEL), 0.1),
        "post_ln_b": nrm(ks[13], (DEPTH, D_MODEL), 0.1),
    }


def reference(x, w_in_a, sgu_ln_g, sgu_ln_b, w_spatial, b_spatial, w_out_a, w_kv,
              w_in_b, attn_sinks, rel_bias, w_out_b, post_ln_g, post_ln_b):
    nc = x.shape[1] // CHUNK
    bias, mask = band_bias_and_mask(rel_bias, nc)
    h = x
    k_band = None
    v_band = None
    for layer in range(DEPTH):
        if layer < N_A_LAYERS:
            i = layer
            sub = sgu_branch(h, w_in_a[i], sgu_ln_g[i], sgu_ln_b[i], w_spatial[i],
                             b_spatial[i], w_out_a[i])
        else:
            i = layer - N_A_LAYERS
            if i == 0:
                k_band, v_band = shared_kv_bands(h, w_kv)
            sub = swa_branch(h, k_band, v_band, bias, mask, w_in_b[i], attn_sinks[i], w_out_b[i])
        h = layer_norm(ALPHA * h + sub, post_ln_g[layer], post_ln_b[layer])
    return h
```

```python
from contextlib import ExitStack

import numpy as np
import concourse.bass as bass
import concourse.mybir as mybir
from concourse.bass_utils import run_bass_kernel_spmd

F32 = mybir.dt.float32
BF16 = mybir.dt.bfloat16
AF = mybir.ActivationFunctionType
ALU = mybir.AluOpType

P = 128
D = 1024
AW = 2048
NCH = 32
NA = NCH + 1
N_CORES = 8
ALPHA = float((2.0 * 2) ** 0.25)
EPS = 1e-5
NEG = -1e30


class _Op:
    __slots__ = ("eng", "fn", "stream", "nsig", "deps", "has_dep", "sigval", "name")


class Prog:
    ENGS = ("pe", "act", "dve", "pool", "sp")
    _nsem = 0

    def __init__(self):
        self.ops = []
        self.byeng = {e: [] for e in self.ENGS}
        self.ks = {}

    def op(self, eng, fn, r=(), w=(), dma=None, ndma=1, name=""):
        o = _Op()
        o.eng = eng
        o.fn = fn
        o.name = name
        o.stream = ("dma", dma) if dma else ("eng", eng)
        o.nsig = 16 * ndma if dma else 1
        deps = set()
        for k in r:
            st = self.ks.setdefault(k, ({}, {}))
            deps.update(st[0].values())
        for k in w:
            st = self.ks.setdefault(k, ({}, {}))
            deps.update(st[0].values())
            deps.update(st[1].values())
        for k in r:
            self.ks[k][1][o.stream] = o
        for k in w:
            self.ks[k][0][o.stream] = o
        o.deps = deps
        o.has_dep = False
        for d in deps:
            d.has_dep = True
        self.ops.append(o)
        self.byeng[eng].append(o)
        return o

    def emit(self, nc, semstack, pre_waits=()):
        counts = {}
        for o in self.ops:
            if o.stream[0] == "dma":
                counts[o.stream] = counts.get(o.stream, 0) + o.nsig
                o.sigval = counts[o.stream]
            else:
                if o.has_dep:
                    counts[o.stream] = counts.get(o.stream, 0) + 1
                o.sigval = counts.get(o.stream, 0)
        streams = list(counts.keys())
        if True:
            sems = {}
            for i, s in enumerate(streams):
                Prog._nsem += 1
                sems[s] = semstack.enter_context(nc.semaphore("s%d_%s" % (Prog._nsem, str(s[1])[:12])))
            final = dict(counts)
            self.sems = sems
            self.final = final

            def run_engine(ename, e):
                waited = {}
                issued = set()
                for (psem, pval) in pre_waits:
                    e.wait_ge(psem, pval)
                for o in self.byeng[ename]:
                    needs = {}
                    for d in o.deps:
                        if d.stream == o.stream and ename == "pe" and o.stream[0] == "eng":
                            continue
                        if d.sigval > needs.get(d.stream, 0):
                            needs[d.stream] = d.sigval
                    for s, v in needs.items():
                        if waited.get(s, 0) >= v:
                            continue
                        e.wait_ge(sems[s], v)
                        waited[s] = v
                    res = o.fn(e)
                    if o.stream[0] == "dma":
                        issued.add(o.stream)
                        for ins in res:
                            ins.then_inc(sems[o.stream], 16)
                    elif o.has_dep:
                        res.then_inc(sems[o.stream], 1)
                for s in issued:
                    e.wait_ge(sems[s], final[s])

            with nc.Block() as block:
                @block.tensor
                def _(e):
                    run_engine("pe", e)

                @block.scalar
                def _(e):
                    run_engine("act", e)

                @block.vector
                def _(e):
                    run_engine("dve", e)

                @block.gpsimd
                def _(e):
                    run_engine("pool", e)

                @block.sync
                def _(e):
                    run_engine("sp", e)


class PsumRing:
    def __init__(self, ps):
        self.ps = ps
        self.ptr = 0

    def take(self, nbanks):
        if nbanks == 2 and self.ptr % 2:
            self.ptr += 1
        b = self.ptr % 8
        self.ptr += nbanks
        ap = self.ps[:, b * 512:(b + nbanks) * 512]
        keys = [("ps", b + i) for i in range(nbanks)]
        return ap, keys


def mm_group(e, out, pairs):
    n = len(pairs)
    ins = None
    for i, (l, r) in enumerate(pairs):
        ins = e.matmul(out, l, r, start=(i == 0), stop=(i == n - 1))
    return ins


def ln_epilogue(pg, pfx, c, src_ps, src_keys, resid, resid_key, ob, stt, Gp, Bp, neghalf, store_fn, store_stream,
                aff_eng="pool"):
    stats, mv, rstd, nmr = stt
    kob = (pfx + "ob",)

    def eR():
        def f_r(e):
            return e.scalar_tensor_tensor(out=ob[:, :], in0=resid, scalar=ALPHA, in1=src_ps,
                                          op0=ALU.mult, op1=ALU.add)
        pg.op("dve", f_r, r=[resid_key], w=src_keys + [kob], name=pfx + "R%d" % c)

    def eStats():
        def f_st(e):
            e.bn_stats(out=stats[:, 0, :], in_=ob[:, 0:512])
            return e.bn_stats(out=stats[:, 1, :], in_=ob[:, 512:1024])
        pg.op("dve", f_st, r=[kob], w=[(pfx + "rstats",)])

        def f_ag(e):
            return e.bn_aggr(out=mv[:, :], in_=stats[:, 0:2, :].rearrange("p a b -> p (a b)"))
        pg.op("dve", f_ag, r=[(pfx + "rstats",)], w=[(pfx + "rmv",)])

        def f_ve(e):
            return e.tensor_scalar(out=rstd[:, :], in0=mv[:, 1:2], scalar1=EPS, scalar2=None, op0=ALU.add)
        pg.op("dve", f_ve, r=[(pfx + "rmv",)], w=[(pfx + "rrstd",)])

        def f_rs(e):
            return e.tensor_tensor(out=rstd[:, :], in0=rstd[:, :], in1=neghalf[:, :], op=ALU.pow)
        pg.op("pool", f_rs, r=[("neghalf",)], w=[(pfx + "rrstd",)])

    def eNmr():
        def f_nm(e):
            return e.scalar_tensor_tensor(out=nmr[:, :], in0=mv[:, 0:1], scalar=-1.0, in1=rstd[:, :],
                                          op0=ALU.mult, op1=ALU.mult)
        pg.op("dve", f_nm, r=[(pfx + "rmv",), (pfx + "rrstd",)], w=[(pfx + "rnmr",)])

    def eNorm():
        def f_n(e):
            return e.activation(out=ob[:, :], in_=ob[:, :], func=AF.Identity,
                                bias=nmr[:, 0:1], scale=rstd[:, 0:1])
        pg.op("act", f_n, r=[(pfx + "rrstd",), (pfx + "rnmr",)], w=[kob])

    def eG():
        def f_g(e):
            return e.tensor_tensor(out=ob[:, :], in0=ob[:, :], in1=Gp[:, :], op=ALU.mult)
        pg.op(aff_eng, f_g, r=[("Gp",)], w=[kob])

    def eB():
        def f_b(e):
            return e.tensor_tensor(out=ob[:, :], in0=ob[:, :], in1=Bp[:, :], op=ALU.add)
        pg.op(aff_eng, f_b, r=[("Bp",)], w=[kob])

        pg.op("sp", store_fn, r=[kob], dma=store_stream, name=pfx + "store%d" % c)

    return eR, eStats, eNmr, eNorm, eG, eB


def b_views(arena):
    flat = arena[:, :, :].rearrange("p a b -> p (a b)")
    kv = lambda a: flat[:, a:a + 8192].rearrange("p (k n) -> p k n", k=8)
    wq, wz, wo = kv(0), kv(8192), kv(16384)
    wk = flat[:, 24576:25600].rearrange("p (k n) -> p k n", k=8)
    wv = flat[:, 25600:26624].rearrange("p (k n) -> p k n", k=8)
    bias = flat[:, 26624:34816].bitcast(F32).rearrange("p (b x) -> p b x", b=2)
    return wq, wk, wv, wz, wo, bias, flat


def b_small_views(arena):
    flat = arena[:, :, :].rearrange("p a b -> p (a b)")
    hb0 = flat[:, 34816:35840]
    hb1 = flat[:, 35840:36864]
    Gp = flat[:, 36864:38912].bitcast(F32)
    Bp = flat[:, 38912:40960].bitcast(F32)
    es = flat[:, 40960:40992].bitcast(F32)
    flag = flat[:, 40992:40994].bitcast(F32)
    ident = flat[:, 41024:41152]
    return hb0, hb1, Gp, Bp, es, flag, ident


def b_small_loads(pg, T, arena):
    hb0, hb1, Gp, Bp, es, flag, ident = b_small_views(arena)

    def op(eng, dst, src, key, stream):
        def f(e):
            return [e.dma_start(out=dst, in_=src)]
        pg.op(eng, f, r=[("arena_free",)], w=[key], dma=stream)

    op("pool", ident, T["ident"], ("b_ident",), "cS_ident")
    op("pool", hb0, T["h1"][0:P, :], ("b_hb", 0), "cS_hb0")
    op("pool", hb1, T["h1"][P:2 * P, :], ("b_hb", 1), "cS_hb1")
    op("sp", Gp, T["Gp1"], ("b_Gp",), "cS_Gp")
    op("sp", Bp, T["Bp1"], ("b_Bp",), "cS_Bp")
    op("sp", es, T["sinks"], ("b_es",), "cS_es")
    op("sp", flag, T["flag"], ("b_flag",), "cS_flag")


_BW = (("w_q", "wq_bf"), ("w_z", "wz_bf"), ("w_out_b", "wo_bf"))


def b_convert(pg, T):
    for src, dst in _BW:
        def f(e, src=src, dst=dst):
            return [e.dma_start(out=T[dst], in_=T[src].rearrange("(kb p) n -> p kb n", p=P))]
        pg.op("pool", f, w=[(dst,)], dma="cv_" + dst)

    def fkv(e):
        return [e.dma_start(out=T["wkv_bf"], in_=T["w_kv"].rearrange("(kb p) (t n) -> p t kb n", p=P, t=2))]
    pg.op("pool", fkv, w=[("wkv_bf",)], dma="cv_wkv_bf")


def b_weight_loads(pg, T, arena, from_bf, extra_keys, extra_r=()):
    wq, wk, wv, wz, wo, bias, flat = b_views(arena)

    def op(eng, dst, src, rkeys, wkeys, stream):
        def f(e):
            return [e.dma_start(out=dst, in_=src)]
        pg.op(eng, f, r=list(rkeys) + list(extra_r), w=list(wkeys) + list(extra_keys), dma=stream)

    if from_bf:
        op("act", flat[:, 24576:26624], T["wkv_bf"].rearrange("p t k n -> p (t k n)"), [("wkv_bf",)],
           [("wk",), ("wv",)], "cB_wkv")
        op("act", wq, T["wq_bf"], [("wq_bf",)], [("wq", 0), ("wq", 1)], "cB_wq")
        op("act", wz, T["wz_bf"], [("wz_bf",)], [("wz", 0), ("wz", 1)], "cB_wz")
        op("act", bias, T["biasT"], [], [("biasT",)], "cB_bias")
        op("act", wo, T["wo_bf"], [("wo_bf",)], [("wo", 0), ("wo", 1)], "cB_wo")
    else:
        kbv = lambda name: T[name].rearrange("(kb p) n -> p kb n", p=P)
        op("pool", wk, kbv("w_kv")[:, :, 0:P], [], [("wk",)], "cB_wk")
        op("pool", wv, kbv("w_kv")[:, :, P:2 * P], [], [("wv",)], "cB_wv")
        for nm, v, key in (("w_q", wq, "wq"), ("w_z", wz, "wz")):
            for nh in range(2):
                op("pool", v[:, :, nh * 512:(nh + 1) * 512], kbv(nm)[:, :, nh * 512:(nh + 1) * 512], [],
                   [(key, nh)], "cB_%s%d" % (key, nh))
        op("sp", bias, T["biasT"], [], [("biasT",)], "cB_bias")
        for nh in range(2):
            op("pool", wo[:, :, nh * 512:(nh + 1) * 512], kbv("w_out_b")[:, :, nh * 512:(nh + 1) * 512], [],
               [("wo", nh)], "cB_wo%d" % nh)


def phase_A(nc, T, ps, semstack, arena, prefetchB=False):
    pg = Prog()
    ring = PsumRing(ps)
    with ExitStack() as es:
        def sb(name, shape, dtype):
            return es.enter_context(nc.sbuf_tensor(name, shape, dtype))

        w_in = arena
        w_out = sb("a_w_out", [P, 16, 1024], BF16)
        Gv = sb("a_Gv", [P, AW], F32)
        Bv = sb("a_Bv", [P, AW], F32)
        Gp = sb("a_Gp", [P, D], F32)
        Bp = sb("a_Bp", [P, D], F32)
        wsT = sb("a_wsT", [P, 8, P], BF16)
        msk = sb("a_msk", [P, P], BF16)
        bspT = sb("a_bspT", [P, 8], F32)
        ident = sb("a_ident", [P, P], BF16)
        xb = [sb("a_xb%d" % i, [P, D], BF16) for i in range(2)]
        hTs = [sb("a_hT%d" % i, [P, D], BF16) for i in range(2)]
        vhat = sb("a_vhat", [P, AW], F32)
        vn = sb("a_vn", [P, AW], BF16)
        sz = sb("a_sz", [P, AW], F32)
        y = sb("a_y", [P, AW], BF16)
        yT = sb("a_yT", [P, AW], BF16)
        xr = [sb("a_xr%d" % i, [P, D], F32) for i in range(2)]
        ob = sb("a_ob", [P, D], F32)
        vstats = sb("a_vstats", [P, 4, 6], F32)
        vmv = sb("a_vmv", [P, 2], F32)
        vrstd = sb("a_vrstd", [P, 1], F32)
        vnmr = sb("a_vnmr", [P, 1], F32)
        rstats = sb("a_rstats", [P, 2, 6], F32)
        rmv = sb("a_rmv", [P, 2], F32)
        rrstd = sb("a_rrstd", [P, 1], F32)
        rnmr = sb("a_rnmr", [P, 1], F32)
        neghalf = sb("a_neghalf", [P, 1], F32)

        def f_nh(e):
            return e.memset(neghalf[:, :], -0.5)
        pg.op("pool", f_nh, w=[("neghalf",)])

        x_d = T["x"]
        h1_d = T["h1"]
        w_in_d = T["w_in_a"].rearrange("(kb p) n -> p kb n", p=P)
        w_out_d = T["w_out_a"].rearrange("(kb p) n -> p kb n", p=P)

        def ld(dst, src, key, stream, eng="pool", after=()):
            def f(e):
                return [e.dma_start(out=dst, in_=src)]
            pg.op(eng, f, r=list(after), w=[key], dma=stream)

        ld(ident[:, :], T["ident"], ("ident",), "c_ident")
        ld(wsT[:, :, :], T["wsT"], ("wsT",), "c_wsT")
        ld(msk[:, :], T["trilT"], ("msk",), "c_msk")
        ld(bspT[:, :], T["bspT"], ("bspT",), "c_bsp", eng="sp")

        def f_mask(e):
            ins = None
            for g in range(8):
                ins = e.tensor_tensor(out=wsT[:, g, :], in0=wsT[:, g, :], in1=msk[:, :], op=ALU.mult)
            return ins
        pg.op("dve", f_mask, r=[("msk",)], w=[("wsT",)])

        def load_xb(c):
            s = c % 2

            def f(e):
                return [e.dma_start(out=xb[s][:, :], in_=x_d[c * P:(c + 1) * P, :])]
            pg.op("pool", f, w=[("xb", s)], dma="xb%d" % s, name="ldxb%d" % c)

        load_xb(0)
        load_xb(1)
        for nb in (4, 5, 6, 7, 8, 9, 10, 11, 0, 1, 2, 3):
            ld(w_in[:, :, nb * 512:(nb + 1) * 512], w_in_d[:, :, nb * 512:(nb + 1) * 512],
               ("w_in", nb), "w_in%d" % nb)
            if nb == 7:
                ld(Gv[:, :], T["Gv"], ("Gv",), "c_Gv", eng="sp", after=[("w_in", 5)])
                ld(Bv[:, :], T["Bv"], ("Bv",), "c_Bv", eng="sp", after=[("w_in", 5)])
        for nh in range(2):
            ld(w_out[:, :, nh * 512:(nh + 1) * 512], w_out_d[:, :, nh * 512:(nh + 1) * 512],
               ("w_out", nh), "w_out%d" % nh)
        ld(Gp[:, :], T["Gp0"], ("Gp",), "c_Gp", eng="sp", after=[("w_in", 1)])
        ld(Bp[:, :], T["Bp0"], ("Bp",), "c_Bp", eng="sp", after=[("w_in", 1)])

        def st_Th(c):
            s = c % 2
            ap, keys = ring.take(2)
            psb = ap.bitcast(BF16).rearrange("p (j q) -> p j q", q=P)

            def f(e):
                ins = None
                for kb in range(8):
                    ins = e.transpose(out=psb[:, kb, :], in_=xb[s][:, kb * P:(kb + 1) * P], identity=ident[:, :])
                return ins
            pg.op("pe", f, r=[("xb", s), ("ident",)], w=keys, name="Th%d" % c)

            def fe(e):
                return e.activation(out=hTs[s][:, :], in_=ap.bitcast(BF16)[:, 0:D], func=AF.Copy)
            pg.op("act", fe, w=keys + [("hT", s)], name="Eh%d" % c)
            if c + 2 < NA:
                load_xb(c + 2)

        def in_mm(c, nb0, tag, extra_w=()):
            ap, keys = ring.take(2)
            hT = hTs[c % 2]

            def f(e):
                ins = None
                for i in range(2):
                    nb = nb0 + i
                    ins = mm_group(e, ap[:, i * 512:(i + 1) * 512],
                                   [(hT[:, kb * P:(kb + 1) * P], w_in[:, kb, nb * 512:(nb + 1) * 512])
                                    for kb in range(8)])
                return ins
            pg.op("pe", f, r=[("hT", c % 2), ("w_in", nb0), ("w_in", nb0 + 1)], w=keys + list(extra_w),
                  name="%s%d_%d" % (tag, c, nb0))
            return ap, keys

        def st_v(c):
            slots = []
            for h in range(2):
                ap, keys = in_mm(c, 4 + 2 * h, "v")
                slots.append((ap, keys))

                def f(e, ap=ap, h=h):
                    e.bn_stats(out=vstats[:, 2 * h, :], in_=ap[:, 0:512])
                    return e.bn_stats(out=vstats[:, 2 * h + 1, :], in_=ap[:, 512:1024])
                pg.op("dve", f, w=keys + [("vstats", h)])

            def f_ag(e):
                return e.bn_aggr(out=vmv[:, :], in_=vstats[:, :, :].rearrange("p a b -> p (a b)"))
            pg.op("dve", f_ag, r=[("vstats", 0), ("vstats", 1)], w=[("vmv",)])

            def f_ve(e):
                return e.tensor_scalar(out=vrstd[:, :], in0=vmv[:, 1:2], scalar1=EPS, scalar2=None, op0=ALU.add)
            pg.op("dve", f_ve, r=[("vmv",)], w=[("vrstd",)])

            def f_rs(e):
                return e.tensor_tensor(out=vrstd[:, :], in0=vrstd[:, :], in1=neghalf[:, :], op=ALU.pow)
            pg.op("pool", f_rs, r=[("neghalf",)], w=[("vrstd",)])

            def f_nm(e):
                return e.scalar_tensor_tensor(out=vnmr[:, :], in0=vmv[:, 0:1], scalar=-1.0, in1=vrstd[:, :],
                                              op0=ALU.mult, op1=ALU.mult)
            pg.op("dve", f_nm, r=[("vmv",), ("vrstd",)], w=[("vnmr",)])
            for h in range(2):
                ap, keys = slots[h]
                sl = slice(h * 1024, (h + 1) * 1024)

                def f_n(e, ap=ap, sl=sl):
                    return e.activation(out=vhat[:, sl], in_=ap, func=AF.Identity,
                                        bias=vnmr[:, 0:1], scale=vrstd[:, 0:1])
                pg.op("act", f_n, r=[("vrstd",), ("vnmr",)], w=keys + [("vhat", h)])

                def f_g(e, sl=sl):
                    return e.tensor_tensor(out=vhat[:, sl], in0=vhat[:, sl], in1=Gv[:, sl], op=ALU.mult)
                pg.op("pool", f_g, r=[("Gv",)], w=[("vhat", h)])

                def f_b(e, sl=sl):
                    return e.tensor_tensor(out=vn[:, sl], in0=vhat[:, sl], in1=Bv[:, sl], op=ALU.add)
                pg.op("pool", f_b, r=[("Bv",), ("vhat", h)], w=[("vn", h)])

        def st_z(c):
            for h in range(2):
                ap, keys = in_mm(c, 8 + 2 * h, "z")
                sl = slice(h * 1024, (h + 1) * 1024)

                def f(e, ap=ap, sl=sl):
                    return e.activation(out=sz[:, sl], in_=ap, func=AF.Silu)
                pg.op("act", f, w=keys + [("sz", h)])

        def st_u(c):
            for h in range(2):
                xw = [("arena_free",)] if (c == NA - 1 and h == 1) else []
                ap, keys = in_mm(c, 2 * h, "u", xw)
                sl = slice(h * 1024, (h + 1) * 1024)

                def f(e, ap=ap, sl=sl):
                    return e.tensor_tensor(out=sz[:, sl], in0=sz[:, sl], in1=ap, op=ALU.mult)
                pg.op("dve", f, w=keys + [("sz", h)])

        def st_s(c):
            for h in range(2):
                ap, keys = ring.take(2)

                def f(e, ap=ap, h=h):
                    ins = None
                    for gl in range(4):
                        g = 4 * h + gl
                        ins = e.matmul(ap[:, gl * 256:(gl + 1) * 256], wsT[:, g, :], vn[:, g * 256:(g + 1) * 256],
                                       start=True, stop=True)
                    return ins
                pg.op("pe", f, r=[("wsT",), ("vn", h)], w=keys, name="s%d_%d" % (c, h))

                def fy(e, ap=ap, h=h):
                    ins = None
                    for gl in range(4):
                        g = 4 * h + gl
                        fs = slice(g * 256, (g + 1) * 256)
                        ins = e.scalar_tensor_tensor(out=y[:, fs], in0=ap[:, gl * 256:(gl + 1) * 256],
                                                     scalar=bspT[:, g:g + 1], in1=sz[:, fs],
                                                     op0=ALU.add, op1=ALU.mult)
                    return ins
                pg.op("dve", fy, r=[("bspT",), ("sz", h)], w=keys + [("y", h)])

        def st_Ty(c):
            ap, keys = ring.take(2)
            psb = ap.bitcast(BF16).rearrange("p (j q) -> p j q", q=P)

            def f(e):
                ins = None
                for fb in range(16):
                    ins = e.transpose(out=psb[:, fb, :], in_=y[:, fb * P:(fb + 1) * P], identity=ident[:, :])
                return ins
            pg.op("pe", f, r=[("y", 0), ("y", 1), ("ident",)], w=keys, name="Ty%d" % c)

            def fe(e):
                return e.activation(out=yT[:, :], in_=ap.bitcast(BF16), func=AF.Copy)
            pg.op("act", fe, w=keys + [("yT",)])

        def load_xr(c):
            s = c % 2

            def fl(e):
                return [e.dma_start(out=xr[s][:, :], in_=x_d[c * P:(c + 1) * P, :])]
            pg.op("sp", fl, w=[("xr", s)], dma="xr%d" % s)

        def st_o(c):
            ap, keys = ring.take(2)
            s = c % 2

            def f(e):
                ins = None
                for nh in range(2):
                    ins = mm_group(e, ap[:, nh * 512:(nh + 1) * 512],
                                   [(yT[:, fb * P:(fb + 1) * P], w_out[:, fb, nh * 512:(nh + 1) * 512])
                                    for fb in range(16)])
                return ins
            pg.op("pe", f, r=[("yT",), ("w_out", 0), ("w_out", 1)], w=keys, name="o%d" % c)

            def store(e):
                return [e.dma_start(out=h1_d[c * P:(c + 1) * P, :], in_=ob[:, :])]
            eps = ln_epilogue(pg, "a_", c, ap, keys, xr[s][:, :], ("xr", s), ob,
                              (rstats, rmv, rrstd, rnmr), Gp, Bp, neghalf, store, "h1st",
                              aff_eng=("dve" if c == NA - 1 else "pool"))
            eps[0]()
            if c + 2 < NA:
                load_xr(c + 2)
            for ep in eps[1:]:
                ep()

        load_xr(0)
        load_xr(1)
        st_Th(0)
        st_v(0)
        for c in range(NA):
            last = (c == NA - 1)
            if c > 0:
                st_Ty(c - 1)
            st_z(c)
            if c + 1 < NA:
                st_Th(c + 1)
            if c > 0:
                st_o(c - 1)
            st_u(c)
            if prefetchB and last:
                b_small_loads(pg, T, arena)
                b_weight_loads(pg, T, arena, True, [], [("arena_free",)])
            st_s(c)
            if prefetchB and c == 4:
                b_convert(pg, T)
            if c + 1 < NA:
                st_v(c + 1)
        st_Ty(NA - 1)
        st_o(NA - 1)

        pg.emit(nc, semstack)
    return [(pg.sems[("dma", "h1st")], pg.final[("dma", "h1st")])]


def phase_B(nc, T, ps, semstack, arena, pre_waits=(), preloaded=False):
    pg = Prog()
    ring = PsumRing(ps)
    with ExitStack() as es:
        def sb(name, shape, dtype):
            return es.enter_context(nc.sbuf_tensor(name, shape, dtype))

        wq, wk, wv, wz, wo, biasT, _flat = b_views(arena)
        if preloaded:
            _hb0, _hb1, Gp, Bp, es_t, flag, ident = b_small_views(arena)
            hb = [_hb0, _hb1]
        else:
            Gp = sb("b_Gp", [P, D], F32)
            Bp = sb("b_Bp", [P, D], F32)
            ident = sb("b_ident", [P, P], BF16)
            es_t = sb("b_es", [P, 16], F32)
            flag = sb("b_flag", [P, 1], F32)
            hb = [sb("b_hb%d" % i, [P, D], BF16) for i in range(2)]
        h1Ts = [sb("b_h1T%d" % i, [P, D], BF16) for i in range(2)]
        qT = sb("b_qT", [P, D], BF16)
        qtm = sb("b_qtm", [P, D], BF16)
        kT = [sb("b_kT%d" % i, [P, P], BF16) for i in range(3)]
        va = [sb("b_va%d" % i, [P, 2, 66], BF16) for i in range(3)]
        t1s = [sb("b_t1_%d" % i, [P, D], F32) for i in range(2)]
        tmp = [sb("b_tmp%d" % i, [P, 1024], F32) for i in range(2)]
        eTb = [[sb("b_eT%d_%d" % (i, b), [P, 16 * P], BF16) for b in range(2)] for i in range(2)]
        den = sb("b_den", [P, 16], F32)
        rec = sb("b_rec", [P, 16], F32)
        t2 = sb("b_t2", [P, 16, 64], F32)
        y2 = sb("b_y2", [P, D], BF16)
        y2T = sb("b_y2T", [P, D], BF16)
        hr = [sb("b_hr%d" % i, [P, D], F32) for i in range(2)]
        obs = [sb("b_ob%d" % i, [P, D], F32) for i in range(2)]
        stt2 = [(sb("b_rstats%d" % i, [P, 2, 6], F32), sb("b_rmv%d" % i, [P, 2], F32),
                 sb("b_rrstd%d" % i, [P, 1], F32), sb("b_rnmr%d" % i, [P, 1], F32)) for i in range(2)]
        neghalf = sb("b_neghalf", [P, 1], F32)

        def f_nh(e):
            return e.memset(neghalf[:, :], -0.5)
        pg.op("pool", f_nh, w=[("neghalf",)])

        h1_d = T["h1"]
        out_d = T["out"]

        def ld(dst, src, key, stream, eng="pool"):
            def f(e):
                return [e.dma_start(out=dst, in_=src)]
            pg.op(eng, f, w=[key], dma=stream)

        def load_hb(i):
            s = i % 2

            def f(e):
                return [e.dma_start(out=hb[s][:, :], in_=h1_d[i * P:(i + 1) * P, :])]
            pg.op("pool", f, w=[("hb", s)], dma="hb%d" % s)

        if not preloaded:
            ld(ident[:, :], T["ident"], ("ident",), "c_ident")
            load_hb(0)
            load_hb(1)
        if not preloaded:
            b_weight_loads(pg, T, arena, False, [])
        if not preloaded:
            ld(Gp[:, :], T["Gp1"], ("Gp",), "c_Gp", eng="sp")
            ld(Bp[:, :], T["Bp1"], ("Bp",), "c_Bp", eng="sp")
            ld(es_t[:, :], T["sinks"], ("es",), "c_es", eng="sp")
            ld(flag[:, :], T["flag"], ("flag",), "c_flag", eng="sp")

        def f_es(e):
            return e.activation(out=es_t[:, :], in_=es_t[:, :], func=AF.Exp)
        pg.op("act", f_es, w=[("es",)])

        def f_es2(e):
            return e.tensor_scalar(out=es_t[:, :], in0=es_t[:, :], scalar1=2.0, scalar2=None, op0=ALU.mult)
        pg.op("dve", f_es2, w=[("es",)])

        def f_ones(e):
            ins = None
            for i in range(3):
                ins = e.memset(va[i][:, :, 64:65], 2.0)
            return ins
        pg.op("dve", f_ones, w=[("va1", 0), ("va1", 1), ("va1", 2)])

        def f_flag1(e):
            return e.tensor_scalar(out=va[0][:, :, 64:65], in0=va[0][:, :, 64:65], scalar1=flag[:, 0:1],
                                   scalar2=None, op0=ALU.mult)
        pg.op("dve", f_flag1, r=[("flag",)], w=[("va1", 0)])

        def st_Th(i):
            s = i % 2
            ap, keys = ring.take(1)
            psb = ap.bitcast(BF16).rearrange("p (j q) -> p j q", q=P)

            def f(e):
                ins = None
                for kb in range(8):
                    ins = e.transpose(out=psb[:, kb, :], in_=hb[s][:, kb * P:(kb + 1) * P], identity=ident[:, :])
                return ins
            pg.op("pe", f, r=[("hb", s), ("ident",)], w=keys)

            def fe(e):
                return e.activation(out=h1Ts[s][:, :], in_=ap.bitcast(BF16)[:, 0:D], func=AF.Copy)
            pg.op("act", fe, w=keys + [("h1T", s)])
            if i + 2 < NA:
                load_hb(i + 2)

        def st_kv(i):
            h1T = h1Ts[i % 2]
            s3 = i % 3
            ap, keys = ring.take(1)
            if i == 3:
                def f_re(e):
                    return e.memset(va[0][:, :, 64:65], 2.0)
                pg.op("dve", f_re, w=[("va1", 0)])

            def f(e):
                mm_group(e, ap[:, 0:P], [(wk[:, kb, :], h1T[:, kb * P:(kb + 1) * P]) for kb in range(8)])
                return mm_group(e, ap[:, P:2 * P], [(h1T[:, kb * P:(kb + 1) * P], wv[:, kb, :]) for kb in range(8)])
            pg.op("pe", f, r=[("h1T", i % 2), ("wk",), ("wv",)], w=keys)

            def fk(e):
                return e.activation(out=kT[s3][:, :], in_=ap[:, 0:P], func=AF.Copy)
            pg.op("act", fk, w=keys + [("kT", s3)])

            if i == 0:
                def fv(e):
                    return e.tensor_scalar(out=va[s3][:, :, 0:64],
                                           in0=ap[:, P:2 * P].rearrange("p (a d) -> p a d", a=2),
                                           scalar1=flag[:, 0:1], scalar2=None, op0=ALU.mult)
                pg.op("dve", fv, r=[("flag",)], w=keys + [("va", s3)])
            else:
                def fv(e):
                    return e.activation(out=va[s3][:, :, 0:64],
                                        in_=ap[:, P:2 * P].rearrange("p (a d) -> p a d", a=2), func=AF.Copy)
                pg.op("act", fv, w=keys + [("va", s3)])

        def st_q(i):
            h1T = h1Ts[i % 2]
            ap, keys = ring.take(2)

            def f(e):
                ins = None
                for nh in range(2):
                    ins = mm_group(e, ap[:, nh * 512:(nh + 1) * 512],
                                   [(h1T[:, kb * P:(kb + 1) * P], wq[:, kb, nh * 512:(nh + 1) * 512])
                                    for kb in range(8)])
                return ins
            pg.op("pe", f, r=[("h1T", i % 2), ("wq", 0), ("wq", 1)], w=keys)

            def fe(e):
                return e.activation(out=qtm[:, :], in_=ap, func=AF.Copy, scale=0.125)
            pg.op("act", fe, w=keys + [("qtm",)])

        def st_qT(i):
            ap, keys = ring.take(1)
            psb = ap.bitcast(BF16).rearrange("p (j q) -> p j q", q=P)

            def f(e):
                ins = None
                for j in range(8):
                    ins = e.transpose(out=psb[:, j, :], in_=qtm[:, j * P:(j + 1) * P], identity=ident[:, :])
                return ins
            pg.op("pe", f, r=[("qtm",), ("ident",)], w=keys)

            def fe(e):
                return e.activation(out=qT[:, :], in_=ap.bitcast(BF16)[:, 0:D], func=AF.Copy)
            pg.op("act", fe, w=keys + [("qT",)])

        def st_z(i):
            h1T = h1Ts[i % 2]
            ap, keys = ring.take(2)
            t1 = t1s[i % 2]
            kt1 = ("t1", i % 2)

            def f(e):
                ins = None
                for nh in range(2):
                    ins = mm_group(e, ap[:, nh * 512:(nh + 1) * 512],
                                   [(h1T[:, kb * P:(kb + 1) * P], wz[:, kb, nh * 512:(nh + 1) * 512])
                                    for kb in range(8)])
                return ins
            pg.op("pe", f, r=[("h1T", i % 2), ("wz", 0), ("wz", 1)], w=keys)

            def ft(e):
                return e.activation(out=t1[:, :], in_=ap, func=AF.Tanh, scale=0.5)
            pg.op("act", ft, w=keys + [kt1])

            def f1(e):
                return e.scalar_tensor_tensor(out=t1[:, :], in0=t1[:, :], scalar=1.0, in1=ap,
                                              op0=ALU.add, op1=ALU.mult)
            pg.op("dve", f1, w=keys + [kt1])

        tmp_ctr = [0]

        def st_scores(i):
            es_ = i % 2
            for b in range(2):
                ks = (i - 1 + b) % 3
                for half in range(2):
                    ap, keys = ring.take(2)
                    h0 = half * 8
                    rows = slice(half * 64, (half + 1) * 64)

                    def f(e, ap=ap, ks=ks, rows=rows):
                        e.matmul(ap[:, 0:512], kT[ks][rows, :], qT[rows, 0:512], start=True, stop=True)
                        return e.matmul(ap[:, 512:1024], kT[ks][rows, :], qT[rows, 512:1024],
                                        start=True, stop=True)
                    pg.op("pe", f, r=[("kT", ks), ("qT",)], w=keys)
                    ts = tmp_ctr[0] % 2
                    tmp_ctr[0] += 1

                    def fb(e, ap=ap, b=b, h0=h0, ts=ts):
                        return e.tensor_tensor(out=tmp[ts][:, :], in0=ap,
                                               in1=biasT[:, b, h0 * P:(h0 + 8) * P], op=ALU.add)
                    pg.op("dve", fb, r=[("biasT",)], w=keys + [("tmp", ts)])

                    def fx(e, b=b, h0=h0, ts=ts, es_=es_):
                        return e.activation(out=eTb[es_][b][:, h0 * P:(h0 + 8) * P],
                                            in_=tmp[ts][:, :], func=AF.Exp)
                    pg.op("act", fx, r=[("tmp", ts)], w=[("eT", es_, b, half)])

        deferred = []

        def flush_y2():
            for fy, hg, i in deferred:
                pg.op("pool", fy, r=[("t2", hg), ("t1", i % 2)], w=[("y2", hg)])
            del deferred[:]

        def st_pv(i):
            es_ = i % 2
            for hg in range(2):
                ap, keys = ring.take(2)
                o3 = ap.rearrange("p (h q) -> p h q", q=P)

                def f(e, o3=o3, hg=hg):
                    ins = None
                    for hl in range(8):
                        h = hg * 8 + hl
                        for b in range(2):
                            ks = (i - 1 + b) % 3
                            ins = e.matmul(o3[:, hl, 0:65], eTb[es_][b][:, h * P:(h + 1) * P],
                                           va[ks][:, hg, 0:65], start=(b == 0), stop=(b == 1))
                    return ins
                rd = [("eT", es_, b, hg) for b in range(2)]
                rd += [("va", (i - 1) % 3), ("va", i % 3), ("va1", (i - 1) % 3), ("va1", i % 3)]
                pg.op("pe", f, r=rd, w=keys)

                def fd(e, o3=o3, hg=hg):
                    return e.tensor_tensor(out=den[:, hg * 8:(hg + 1) * 8], in0=o3[:, :, 64],
                                           in1=es_t[:, hg * 8:(hg + 1) * 8], op=ALU.add)
                pg.op("dve", fd, r=[("es",)], w=keys + [("den", hg)])

                def fr(e, hg=hg):
                    return e.reciprocal(out=rec[:, hg * 8:(hg + 1) * 8], in_=den[:, hg * 8:(hg + 1) * 8])
                pg.op("dve", fr, r=[("den", hg)], w=[("rec", hg)])

                def f2(e, o3=o3, hg=hg):
                    return e.tensor_tensor(out=t2[:, hg * 8:(hg + 1) * 8, :], in0=o3[:, :, 0:64],
                                           in1=rec[:, hg * 8:(hg + 1) * 8].unsqueeze(2).to_broadcast([P, 8, 64]),
                                           op=ALU.mult)
                pg.op("dve", f2, r=[("rec", hg)], w=keys + [("t2", hg)])

                def fy(e, hg=hg, t1=t1s[i % 2]):
                    sl = slice(hg * 512, (hg + 1) * 512)
                    return e.tensor_tensor(out=y2[:, sl],
                                           in0=t2[:, hg * 8:(hg + 1) * 8, :].rearrange("p h d -> p (h d)"),
                                           in1=t1[:, sl], op=ALU.mult)
                deferred.append((fy, hg, i))

        def st_Ty(i):
            ap, keys = ring.take(1)
            psb = ap.bitcast(BF16).rearrange("p (j q) -> p j q", q=P)

            def f(e):
                ins = None
                for fb in range(8):
                    ins = e.transpose(out=psb[:, fb, :], in_=y2[:, fb * P:(fb + 1) * P], identity=ident[:, :])
                return ins
            pg.op("pe", f, r=[("y2", 0), ("y2", 1), ("ident",)], w=keys)

            def fe(e):
                return e.activation(out=y2T[:, :], in_=ap.bitcast(BF16)[:, 0:D], func=AF.Copy)
            pg.op("act", fe, w=keys + [("y2T",)])
            s = i % 2

            def fl(e):
                return [e.dma_start(out=hr[s][:, :], in_=h1_d[i * P:(i + 1) * P, :])]
            pg.op("sp", fl, w=[("hr", s)], dma="hr%d" % s)

        def st_o(i):
            c = i - 1
            ap, keys = ring.take(2)
            s = i % 2

            def f(e):
                ins = None
                for nh in range(2):
                    ins = mm_group(e, ap[:, nh * 512:(nh + 1) * 512],
                                   [(y2T[:, fb * P:(fb + 1) * P], wo[:, fb, nh * 512:(nh + 1) * 512])
                                    for fb in range(8)])
                return ins
            pg.op("pe", f, r=[("y2T",), ("wo", 0), ("wo", 1)], w=keys)

            ob = obs[i % 2]

            def store(e):
                return [e.dma_start(out=out_d[c * P:(c + 1) * P, :], in_=ob[:, :])]
            return ln_epilogue(pg, "b%d_" % (i % 2), c, ap, keys, hr[s][:, :], ("hr", s), ob,
                               stt2[i % 2], Gp, Bp, neghalf, store, "outst", aff_eng="pool")

        st_Th(0)
        st_Th(1)
        st_kv(0)
        st_q(1)
        st_z(1)
        st_qT(1)
        st_kv(1)
        st_Th(2)
        st_scores(1)
        pending = []
        for j in range(1, NA):
            if j + 1 < NA:
                st_q(j + 1)
            flush_y2()
            if pending:
                pending[0]()
                pending[1]()
            if j + 1 < NA:
                st_z(j + 1)
            if j - 1 >= 1:
                st_Ty(j - 1)
            if j + 1 < NA:
                st_qT(j + 1)
            if pending:
                pending[2]()
            if j + 1 < NA:
                st_kv(j + 1)
            if j + 2 < NA:
                st_Th(j + 2)
            st_pv(j)
            late = None
            if pending:
                pending[3]()
                late = pending[4]
            pending = []
            if j - 1 >= 1:
                eps = st_o(j - 1)
                eps[0]()
                pending = list(eps[1:])
            if j + 1 < NA:
                st_scores(j + 1)
            if late:
                late()
        flush_y2()
        st_Ty(NA - 1)
        eps_last = st_o(NA - 1)
        for ep in pending:
            ep()
        for ep in eps_last:
            ep()

        pg.emit(nc, semstack, pre_waits)


def build(mode):
    nc = bass.Bass("TRN2", target_bir_lowering=False)
    T = {}

    def din(name, shape):
        T[name] = nc.dram_tensor(name, list(shape), F32, kind="ExternalInput").ap()

    doA = "A" in mode
    doB = "B" in mode
    din("ident", [P, P])
    if doA:
        din("x", [NA * P, D])
        din("w_in_a", [D, 3 * AW])
        din("w_out_a", [AW, D])
        din("Gv", [P, AW])
        din("Bv", [P, AW])
        din("Gp0", [P, D])
        din("Bp0", [P, D])
        din("wsT", [P, 8, P])
        din("trilT", [P, P])
        din("bspT", [P, 8])
    if doB:
        din("w_kv", [D, 2 * P])
        din("w_q", [D, D])
        din("w_z", [D, D])
        din("w_out_b", [D, D])
        din("biasT", [P, 2, 16 * P])
        din("Gp1", [P, D])
        din("Bp1", [P, D])
        din("sinks", [P, 16])
        din("flag", [P, 1])
        T["out"] = nc.dram_tensor("out", [NCH * P, D], F32, kind="ExternalOutput").ap()
    if doA and doB:
        T["h1"] = nc.dram_tensor("h1", [NA * P, D], F32, kind="Internal").ap()
    elif doA:
        T["h1"] = nc.dram_tensor("h1", [NA * P, D], F32, kind="ExternalOutput").ap()
    else:
        din("h1", [NA * P, D])

    if doA and doB:
        for _, dst in _BW:
            T[dst] = nc.dram_tensor(dst, [P, 8, D], BF16, kind="Internal").ap()
        T["wkv_bf"] = nc.dram_tensor("wkv_bf", [P, 2, 8, P], BF16, kind="Internal").ap()
    with ExitStack() as semstack, nc.psum_tensor("ps", [P, 4096], F32) as ps, \
            nc.sbuf_tensor("arena", [P, 8, 6144], BF16) as arena:
        pre = ()
        if doA:
            pre = phase_A(nc, T, ps, semstack, arena, prefetchB=doB)
        if doB:
            phase_B(nc, T, ps, semstack, arena, pre, preloaded=doA)
    return nc


def _rel_bucket_np(d):
    max_exact = 16
    df = np.maximum(d, 1).astype(np.float32)
    large = max_exact + (np.log(df / np.float32(max_exact)) / np.float32(np.log(128 / max_exact))
                         * np.float32(32 - max_exact)).astype(np.int32)
    large = np.minimum(large, 31)
    return np.where(d < max_exact, d, large)


def _bias_layout(rel_bias):
    j = np.arange(P)[:, None]
    t = np.arange(P)[None, :]
    out = np.empty((P, 2, 16, P), np.float32)
    for b in range(2):
        d = t - j + (P if b == 0 else 0)
        valid = (d >= 0) & (d < P)
        bk = _rel_bucket_np(np.clip(d, 0, P - 1))
        g = rel_bias[bk]
        g = np.transpose(g, (0, 2, 1))
        out[:, b] = np.where(valid[:, None, :], g, np.float32(NEG))
    return np.ascontiguousarray(out.reshape(P, 2, 16 * P))


def _bcast(v, n):
    return np.ascontiguousarray(np.broadcast_to(np.asarray(v, np.float32).reshape(1, n), (P, n)))


def _prep(inputs):
    f = lambda a: np.ascontiguousarray(np.asarray(a, dtype=np.float32))
    x = f(inputs["x"])
    shared = {"ident": np.eye(P, dtype=np.float32)}
    shared["w_in_a"] = f(inputs["w_in_a"][0])
    shared["w_out_a"] = f(inputs["w_out_a"][0])
    shared["Gv"] = _bcast(inputs["sgu_ln_g"][0], AW)
    shared["Bv"] = _bcast(inputs["sgu_ln_b"][0], AW)
    shared["Gp0"] = _bcast(inputs["post_ln_g"][0], D)
    shared["Bp0"] = _bcast(inputs["post_ln_b"][0], D)
    shared["Gp1"] = _bcast(inputs["post_ln_g"][1], D)
    shared["Bp1"] = _bcast(inputs["post_ln_b"][1], D)
    ws = f(inputs["w_spatial"][0])
    shared["wsT"] = np.ascontiguousarray(np.transpose(ws, (2, 0, 1)))
    shared["trilT"] = np.ascontiguousarray(np.triu(np.ones((P, P), np.float32)))
    shared["bspT"] = np.ascontiguousarray(f(inputs["b_spatial"][0]).T)
    shared["w_kv"] = f(inputs["w_kv"])
    wb = f(inputs["w_in_b"][0])
    wq = wb[:, :D].reshape(D, 2, 8, 64)
    shared["w_q"] = np.ascontiguousarray(np.transpose(wq, (0, 2, 1, 3)).reshape(D, D))
    shared["w_z"] = np.ascontiguousarray(wb[:, D:])
    shared["w_out_b"] = f(inputs["w_out_b"][0])
    shared["biasT"] = _bias_layout(f(inputs["rel_bias"]))
    shared["sinks"] = _bcast(inputs["attn_sinks"][0], 16)
    per_core = []
    half = NCH * P
    for core in range(N_CORES):
        bi, hi = core // 2, core % 2
        xc = np.zeros((NA * P, D), np.float32)
        xc[P:] = x[bi, hi * half:(hi + 1) * half]
        if hi == 1:
            xc[:P] = x[bi, half - P:half]
        flag = np.full((P, 1), 1.0 if hi == 1 else 0.0, np.float32)
        per_core.append({"x": xc, "flag": flag})
    return shared, per_core


_A_KEYS = ("ident", "x", "w_in_a", "w_out_a", "Gv", "Bv", "Gp0", "Bp0", "wsT", "trilT", "bspT")
_B_KEYS = ("ident", "w_kv", "w_q", "w_z", "w_out_b", "biasT", "Gp1", "Bp1", "sinks", "flag")

FUSED = True
_NC_CACHE = {}


def _get_nc(mode):
    if mode not in _NC_CACHE:
        _NC_CACHE[mode] = build(mode)
    return _NC_CACHE[mode]


def kernel(**inputs):
    shared, per_core = _prep(inputs)
    cores = list(range(N_CORES))
    if FUSED:
        keys = tuple(dict.fromkeys(_A_KEYS + _B_KEYS))
        in_maps = []
        for c in cores:
            m = {k: (shared[k] if k in shared else per_core[c][k]) for k in keys}
            in_maps.append(m)
        res = run_bass_kernel_spmd(_get_nc("AB"), in_maps, core_ids=cores)
        outs = [np.asarray(r["out"]) for r in res.results]
    else:
        in_maps = [{k: (shared[k] if k in shared else per_core[c][k]) for k in _A_KEYS} for c in cores]
        resA = run_bass_kernel_spmd(_get_nc("A"), in_maps, core_ids=cores)
        in_maps = []
        for c in cores:
            m = {k: (shared[k] if k in shared else per_core[c][k]) for k in _B_KEYS}
            m["h1"] = np.asarray(resA.results[c]["h1"])
            in_maps.append(m)
        resB = run_bass_kernel_spmd(_get_nc("B"), in_maps, core_ids=cores)
        outs = [np.asarray(r["out"]) for r in resB.results]
    out = np.empty((4, 2 * NCH * P, D), np.float32)
    for c in cores:
        out[c // 2, (c % 2) * NCH * P:((c % 2) + 1) * NCH * P] = outs[c].reshape(NCH * P, D)
    return out
```

```python
from contextlib import ExitStack

import numpy as np
import concourse.bass as bass
import concourse.mybir as mybir
from concourse.bass_utils import run_bass_kernel_spmd

F32 = mybir.dt.float32
BF16 = mybir.dt.bfloat16
AF = mybir.ActivationFunctionType
ALU = mybir.AluOpType

P = 128
D = 1024
AW = 2048
NCH = 32
NA = NCH + 1
N_CORES = 8
ALPHA = float((2.0 * 2) ** 0.25)
EPS = 1e-5
NEG = -1e30


class _Op:
    __slots__ = ("eng", "fn", "stream", "nsig", "deps", "has_dep", "sigval", "name")


class Prog:
    ENGS = ("pe", "act", "dve", "pool", "sp")
    _nsem = 0

    def __init__(self):
        self.ops = []
        self.byeng = {e: [] for e in self.ENGS}
        self.ks = {}

    def op(self, eng, fn, r=(), w=(), dma=None, ndma=1, name=""):
        o = _Op()
        o.eng = eng
        o.fn = fn
        o.name = name
        o.stream = ("dma", dma) if dma else ("eng", eng)
        o.nsig = 16 * ndma if dma else 1
        deps = set()
        for k in r:
            st = self.ks.setdefault(k, ({}, {}))
            deps.update(st[0].values())
        for k in w:
            st = self.ks.setdefault(k, ({}, {}))
            deps.update(st[0].values())
            deps.update(st[1].values())
        for k in r:
            self.ks[k][1][o.stream] = o
        for k in w:
            self.ks[k][0][o.stream] = o
        o.deps = deps
        o.has_dep = False
        for d in deps:
            d.has_dep = True
        self.ops.append(o)
        self.byeng[eng].append(o)
        return o

    def emit(self, nc, semstack, pre_waits=()):
        counts = {}
        for o in self.ops:
            if o.stream[0] == "dma":
                counts[o.stream] = counts.get(o.stream, 0) + o.nsig
                o.sigval = counts[o.stream]
            else:
                if o.has_dep:
                    counts[o.stream] = counts.get(o.stream, 0) + 1
                o.sigval = counts.get(o.stream, 0)
        streams = list(counts.keys())
        if True:
            sems = {}
            for i, s in enumerate(streams):
                Prog._nsem += 1
                sems[s] = semstack.enter_context(nc.semaphore("s%d_%s" % (Prog._nsem, str(s[1])[:12])))
            final = dict(counts)
            self.sems = sems
            self.final = final

            def run_engine(ename, e):
                waited = {}
                issued = set()
                for (psem, pval) in pre_waits:
                    e.wait_ge(psem, pval)
                for o in self.byeng[ename]:
                    needs = {}
                    for d in o.deps:
                        if d.stream == o.stream and ename == "pe" and o.stream[0] == "eng":
                            continue
                        if d.sigval > needs.get(d.stream, 0):
                            needs[d.stream] = d.sigval
                    for s, v in needs.items():
                        if waited.get(s, 0) >= v:
                            continue
                        e.wait_ge(sems[s], v)
                        waited[s] = v
                    res = o.fn(e)
                    if o.stream[0] == "dma":
                        issued.add(o.stream)
                        for ins in res:
                            ins.then_inc(sems[o.stream], 16)
                    elif o.has_dep:
                        res.then_inc(sems[o.stream], 1)
                for s in issued:
                    e.wait_ge(sems[s], final[s])

            with nc.Block() as block:
                @block.tensor
                def _(e):
                    run_engine("pe", e)

                @block.scalar
                def _(e):
                    run_engine("act", e)

                @block.vector
                def _(e):
                    run_engine("dve", e)

                @block.gpsimd
                def _(e):
                    run_engine("pool", e)

                @block.sync
                def _(e):
                    run_engine("sp", e)


class PsumRing:
    def __init__(self, ps):
        self.ps = ps
        self.ptr = 0

    def take(self, nbanks):
        if nbanks == 2 and self.ptr % 2:
            self.ptr += 1
        b = self.ptr % 8
        self.ptr += nbanks
        ap = self.ps[:, b * 512:(b + nbanks) * 512]
        keys = [("ps", b + i) for i in range(nbanks)]
        return ap, keys


def mm_group(e, out, pairs):
    n = len(pairs)
    ins = None
    for i, (l, r) in enumerate(pairs):
        ins = e.matmul(out, l, r, start=(i == 0), stop=(i == n - 1))
    return ins


def ln_epilogue(pg, pfx, c, src_ps, src_keys, resid, resid_key, ob, stt, Gp, Bp, neghalf, store_fn, store_stream,
                aff_eng="pool", split_r=False):
    stats, mv, rstd, nmr = stt
    kob = (pfx + "ob",)

    def eR():
        if split_r:
            for bk in range(2):
                def f_r(e, bk=bk):
                    sl = slice(bk * 512, (bk + 1) * 512)
                    return e.scalar_tensor_tensor(out=ob[:, sl], in0=resid[:, sl], scalar=ALPHA,
                                                  in1=src_ps[:, sl], op0=ALU.mult, op1=ALU.add)
                pg.op("dve", f_r, r=[resid_key], w=[src_keys[bk], kob], name=pfx + "R%d_%d" % (c, bk))
            return

        def f_r(e):
            return e.scalar_tensor_tensor(out=ob[:, :], in0=resid, scalar=ALPHA, in1=src_ps,
                                          op0=ALU.mult, op1=ALU.add)
        pg.op("dve", f_r, r=[resid_key], w=src_keys + [kob], name=pfx + "R%d" % c)

    def eStats():
        def f_st(e):
            e.bn_stats(out=stats[:, 0, :], in_=ob[:, 0:512])
            return e.bn_stats(out=stats[:, 1, :], in_=ob[:, 512:1024])
        pg.op("dve", f_st, r=[kob], w=[(pfx + "rstats",)])

        def f_ag(e):
            return e.bn_aggr(out=mv[:, :], in_=stats[:, 0:2, :].rearrange("p a b -> p (a b)"))
        pg.op("dve", f_ag, r=[(pfx + "rstats",)], w=[(pfx + "rmv",)])

        def f_ve(e):
            return e.tensor_scalar(out=rstd[:, :], in0=mv[:, 1:2], scalar1=EPS, scalar2=None, op0=ALU.add)
        pg.op("dve", f_ve, r=[(pfx + "rmv",)], w=[(pfx + "rrstd",)])

        def f_rs(e):
            return e.tensor_tensor(out=rstd[:, :], in0=rstd[:, :], in1=neghalf[:, :], op=ALU.pow)
        pg.op("pool", f_rs, r=[("neghalf",)], w=[(pfx + "rrstd",)])

    def eNmr():
        def f_nm(e):
            return e.scalar_tensor_tensor(out=nmr[:, :], in0=mv[:, 0:1], scalar=-1.0, in1=rstd[:, :],
                                          op0=ALU.mult, op1=ALU.mult)
        pg.op("dve", f_nm, r=[(pfx + "rmv",), (pfx + "rrstd",)], w=[(pfx + "rnmr",)])

    def eNorm():
        def f_n(e):
            return e.activation(out=ob[:, :], in_=ob[:, :], func=AF.Identity,
                                bias=nmr[:, 0:1], scale=rstd[:, 0:1])
        pg.op("act", f_n, r=[(pfx + "rrstd",), (pfx + "rnmr",)], w=[kob])

    def eG():
        def f_g(e):
            return e.tensor_tensor(out=ob[:, :], in0=ob[:, :], in1=Gp[:, :], op=ALU.mult)
        pg.op(aff_eng, f_g, r=[("Gp",)], w=[kob])

    def eB():
        def f_b(e):
            return e.tensor_tensor(out=ob[:, :], in0=ob[:, :], in1=Bp[:, :], op=ALU.add)
        pg.op(aff_eng, f_b, r=[("Bp",)], w=[kob])

        pg.op("sp", store_fn, r=[kob], dma=store_stream, name=pfx + "store%d" % c)

    return eR, eStats, eNmr, eNorm, eG, eB


def b_views(arena):
    flat = arena[:, :, :].rearrange("p a b -> p (a b)")
    kv = lambda a: flat[:, a:a + 8192].rearrange("p (k n) -> p k n", k=8)
    wq, wz, wo = kv(0), kv(8192), kv(16384)
    wk = flat[:, 24576:25600].rearrange("p (k n) -> p k n", k=8)
    wv = flat[:, 25600:26624].rearrange("p (k n) -> p k n", k=8)
    bias = flat[:, 26624:34816].bitcast(F32).rearrange("p (b x) -> p b x", b=2)
    return wq, wk, wv, wz, wo, bias, flat


def b_small_views(arena):
    flat = arena[:, :, :].rearrange("p a b -> p (a b)")
    hb0 = flat[:, 34816:35840]
    hb1 = flat[:, 35840:36864]
    Gp = flat[:, 36864:38912].bitcast(F32)
    Bp = flat[:, 38912:40960].bitcast(F32)
    es = flat[:, 40960:40992].bitcast(F32)
    flag = flat[:, 40992:40994].bitcast(F32)
    ident = flat[:, 41024:41152]
    return hb0, hb1, Gp, Bp, es, flag, ident


def b_small_loads(pg, T, arena):
    hb0, hb1, Gp, Bp, es, flag, ident = b_small_views(arena)

    def op(eng, dst, src, key, stream):
        def f(e):
            return [e.dma_start(out=dst, in_=src)]
        pg.op(eng, f, r=[("arena_free",)], w=[key], dma=stream)

    op("pool", ident, T["ident"], ("b_ident",), "cS_ident")
    op("pool", hb0, T["h1"][0:P, :], ("b_hb", 0), "cS_hb0")
    op("pool", hb1, T["h1"][P:2 * P, :], ("b_hb", 1), "cS_hb1")
    op("sp", Gp, T["Gp1"], ("b_Gp",), "cS_Gp")
    op("sp", Bp, T["Bp1"], ("b_Bp",), "cS_Bp")
    op("sp", es, T["sinks"], ("b_es",), "cS_es")
    op("sp", flag, T["flag"], ("b_flag",), "cS_flag")


_BW = (("w_q", "wq_bf"), ("w_z", "wz_bf"), ("w_out_b", "wo_bf"))


def b_convert(pg, T):
    for src, dst in _BW:
        def f(e, src=src, dst=dst):
            return [e.dma_start(out=T[dst], in_=T[src].rearrange("(kb p) n -> p kb n", p=P))]
        pg.op("pool", f, w=[(dst,)], dma="cv_" + dst)

    def fkv(e):
        return [e.dma_start(out=T["wkv_bf"], in_=T["w_kv"].rearrange("(kb p) (t n) -> p t kb n", p=P, t=2))]
    pg.op("pool", fkv, w=[("wkv_bf",)], dma="cv_wkv_bf")


def b_weight_loads(pg, T, arena, from_bf, extra_keys, extra_r=()):
    wq, wk, wv, wz, wo, bias, flat = b_views(arena)

    def op(eng, dst, src, rkeys, wkeys, stream):
        def f(e):
            return [e.dma_start(out=dst, in_=src)]
        pg.op(eng, f, r=list(rkeys) + list(extra_r), w=list(wkeys) + list(extra_keys), dma=stream)

    if from_bf:
        op("act", flat[:, 24576:26624], T["wkv_bf"].rearrange("p t k n -> p (t k n)"), [("wkv_bf",)],
           [("wk",), ("wv",)], "cB_wkv")
        op("act", wq, T["wq_bf"], [("wq_bf",)], [("wq", 0), ("wq", 1)], "cB_wq")
        op("act", wz, T["wz_bf"], [("wz_bf",)], [("wz", 0), ("wz", 1)], "cB_wz")
        op("act", bias, T["biasT"], [], [("biasT",)], "cB_bias")
        op("act", wo, T["wo_bf"], [("wo_bf",)], [("wo", 0), ("wo", 1)], "cB_wo")
    else:
        kbv = lambda name: T[name].rearrange("(kb p) n -> p kb n", p=P)
        op("pool", wk, kbv("w_kv")[:, :, 0:P], [], [("wk",)], "cB_wk")
        op("pool", wv, kbv("w_kv")[:, :, P:2 * P], [], [("wv",)], "cB_wv")
        for nm, v, key in (("w_q", wq, "wq"), ("w_z", wz, "wz")):
            for nh in range(2):
                op("pool", v[:, :, nh * 512:(nh + 1) * 512], kbv(nm)[:, :, nh * 512:(nh + 1) * 512], [],
                   [(key, nh)], "cB_%s%d" % (key, nh))
        op("sp", bias, T["biasT"], [], [("biasT",)], "cB_bias")
        for nh in range(2):
            op("pool", wo[:, :, nh * 512:(nh + 1) * 512], kbv("w_out_b")[:, :, nh * 512:(nh + 1) * 512], [],
               [("wo", nh)], "cB_wo%d" % nh)


def phase_A(nc, T, ps, semstack, arena, prefetchB=False):
    pg = Prog()
    ring = PsumRing(ps)
    with ExitStack() as es:
        def sb(name, shape, dtype):
            return es.enter_context(nc.sbuf_tensor(name, shape, dtype))

        w_in = arena
        w_out = sb("a_w_out", [P, 16, 1024], BF16)
        Gv = sb("a_Gv", [P, AW], F32)
        Bv = sb("a_Bv", [P, AW], F32)
        Gp = sb("a_Gp", [P, D], F32)
        Bp = sb("a_Bp", [P, D], F32)
        wsT = sb("a_wsT", [P, 8, P], BF16)
        msk = sb("a_msk", [P, P], BF16)
        bspT = sb("a_bspT", [P, 8], F32)
        ident = sb("a_ident", [P, P], BF16)
        xb = [sb("a_xb%d" % i, [P, D], BF16) for i in range(2)]
        hTs = [sb("a_hT%d" % i, [P, D], BF16) for i in range(2)]
        vhat = sb("a_vhat", [P, AW], F32)
        vn = sb("a_vn", [P, AW], BF16)
        sz = sb("a_sz", [P, AW], F32)
        y = sb("a_y", [P, AW], BF16)
        yT = sb("a_yT", [P, AW], BF16)
        xr = [sb("a_xr%d" % i, [P, D], F32) for i in range(2)]
        ob = sb("a_ob", [P, D], F32)
        vstats = sb("a_vstats", [P, 4, 6], F32)
        vmv = sb("a_vmv", [P, 2], F32)
        vrstd = sb("a_vrstd", [P, 1], F32)
        vnmr = sb("a_vnmr", [P, 1], F32)
        rstats = sb("a_rstats", [P, 2, 6], F32)
        rmv = sb("a_rmv", [P, 2], F32)
        rrstd = sb("a_rrstd", [P, 1], F32)
        rnmr = sb("a_rnmr", [P, 1], F32)
        neghalf = sb("a_neghalf", [P, 1], F32)

        def f_nh(e):
            return e.memset(neghalf[:, :], -0.5)
        pg.op("pool", f_nh, w=[("neghalf",)])

        x_d = T["x"]
        h1_d = T["h1"]
        w_in_d = T["w_in_a"].rearrange("(kb p) n -> p kb n", p=P)
        w_out_d = T["w_out_a"].rearrange("(kb p) n -> p kb n", p=P)

        def ld(dst, src, key, stream, eng="pool", after=()):
            def f(e):
                return [e.dma_start(out=dst, in_=src)]
            pg.op(eng, f, r=list(after), w=[key], dma=stream)

        ld(ident[:, :], T["ident"], ("ident",), "c_ident")
        ld(wsT[:, :, :], T["wsT"], ("wsT",), "c_wsT")
        ld(msk[:, :], T["trilT"], ("msk",), "c_msk")
        ld(bspT[:, :], T["bspT"], ("bspT",), "c_bsp", eng="sp")

        def f_mask(e):
            ins = None
            for g in range(8):
                ins = e.tensor_tensor(out=wsT[:, g, :], in0=wsT[:, g, :], in1=msk[:, :], op=ALU.mult)
            return ins
        pg.op("dve", f_mask, r=[("msk",)], w=[("wsT",)])

        def load_xb(c):
            s = c % 2

            def f(e):
                return [e.dma_start(out=xb[s][:, :], in_=x_d[c * P:(c + 1) * P, :])]
            pg.op("pool", f, w=[("xb", s)], dma="xb%d" % s, name="ldxb%d" % c)

        load_xb(0)
        load_xb(1)
        for nb in (4, 5, 6, 7, 8, 9, 10, 11, 0, 1, 2, 3):
            ld(w_in[:, :, nb * 512:(nb + 1) * 512], w_in_d[:, :, nb * 512:(nb + 1) * 512],
               ("w_in", nb), "w_in%d" % nb)
            if nb == 7:
                ld(Gv[:, :], T["Gv"], ("Gv",), "c_Gv", eng="sp", after=[("w_in", 5)])
                ld(Bv[:, :], T["Bv"], ("Bv",), "c_Bv", eng="sp", after=[("w_in", 5)])
        for nh in range(2):
            ld(w_out[:, :, nh * 512:(nh + 1) * 512], w_out_d[:, :, nh * 512:(nh + 1) * 512],
               ("w_out", nh), "w_out%d" % nh)
        ld(Gp[:, :], T["Gp0"], ("Gp",), "c_Gp", eng="sp", after=[("w_in", 1)])
        ld(Bp[:, :], T["Bp0"], ("Bp",), "c_Bp", eng="sp", after=[("w_in", 1)])

        def st_Th(c):
            s = c % 2
            ap, keys = ring.take(2)
            psb = ap.bitcast(BF16).rearrange("p (j q) -> p j q", q=P)

            def f(e):
                ins = None
                for kb in range(8):
                    ins = e.transpose(out=psb[:, kb, :], in_=xb[s][:, kb * P:(kb + 1) * P], identity=ident[:, :])
                return ins
            pg.op("pe", f, r=[("xb", s), ("ident",)], w=keys, name="Th%d" % c)

            def fe(e):
                return e.activation(out=hTs[s][:, :], in_=ap.bitcast(BF16)[:, 0:D], func=AF.Copy)
            pg.op("act", fe, w=keys + [("hT", s)], name="Eh%d" % c)
            if c + 2 < NA:
                load_xb(c + 2)

        def in_mm(c, nb0, tag, extra_w=()):
            ap, keys = ring.take(2)
            hT = hTs[c % 2]

            def f(e):
                ins = None
                for i in range(2):
                    nb = nb0 + i
                    ins = mm_group(e, ap[:, i * 512:(i + 1) * 512],
                                   [(hT[:, kb * P:(kb + 1) * P], w_in[:, kb, nb * 512:(nb + 1) * 512])
                                    for kb in range(8)])
                return ins
            pg.op("pe", f, r=[("hT", c % 2), ("w_in", nb0), ("w_in", nb0 + 1)], w=keys + list(extra_w),
                  name="%s%d_%d" % (tag, c, nb0))
            return ap, keys

        def st_v(c):
            slots = []
            for h in range(2):
                ap, keys = in_mm(c, 4 + 2 * h, "v")
                slots.append((ap, keys))

                def f(e, ap=ap, h=h):
                    e.bn_stats(out=vstats[:, 2 * h, :], in_=ap[:, 0:512])
                    return e.bn_stats(out=vstats[:, 2 * h + 1, :], in_=ap[:, 512:1024])
                pg.op("dve", f, w=keys + [("vstats", h)])

            def f_ag(e):
                return e.bn_aggr(out=vmv[:, :], in_=vstats[:, :, :].rearrange("p a b -> p (a b)"))
            pg.op("dve", f_ag, r=[("vstats", 0), ("vstats", 1)], w=[("vmv",)])

            def f_ve(e):
                return e.tensor_scalar(out=vrstd[:, :], in0=vmv[:, 1:2], scalar1=EPS, scalar2=None, op0=ALU.add)
            pg.op("dve", f_ve, r=[("vmv",)], w=[("vrstd",)])

            def f_rs(e):
                return e.tensor_tensor(out=vrstd[:, :], in0=vrstd[:, :], in1=neghalf[:, :], op=ALU.pow)
            pg.op("pool", f_rs, r=[("neghalf",)], w=[("vrstd",)])

            def f_nm(e):
                return e.scalar_tensor_tensor(out=vnmr[:, :], in0=vmv[:, 0:1], scalar=-1.0, in1=vrstd[:, :],
                                              op0=ALU.mult, op1=ALU.mult)
            pg.op("dve", f_nm, r=[("vmv",), ("vrstd",)], w=[("vnmr",)])
            for h in range(2):
                ap, keys = slots[h]
                sl = slice(h * 1024, (h + 1) * 1024)

                def f_n(e, ap=ap, sl=sl):
                    return e.activation(out=vhat[:, sl], in_=ap, func=AF.Identity,
                                        bias=vnmr[:, 0:1], scale=vrstd[:, 0:1])
                pg.op("act", f_n, r=[("vrstd",), ("vnmr",)], w=keys + [("vhat", h)])

                def f_g(e, sl=sl):
                    return e.tensor_tensor(out=vhat[:, sl], in0=vhat[:, sl], in1=Gv[:, sl], op=ALU.mult)
                pg.op("pool", f_g, r=[("Gv",)], w=[("vhat", h)])

                def f_b(e, sl=sl):
                    return e.tensor_tensor(out=vn[:, sl], in0=vhat[:, sl], in1=Bv[:, sl], op=ALU.add)
                pg.op("pool", f_b, r=[("Bv",), ("vhat", h)], w=[("vn", h)])

        def st_z(c):
            for h in range(2):
                ap, keys = in_mm(c, 8 + 2 * h, "z")
                sl = slice(h * 1024, (h + 1) * 1024)

                def f(e, ap=ap, sl=sl):
                    return e.activation(out=sz[:, sl], in_=ap, func=AF.Silu)
                pg.op("act", f, w=keys + [("sz", h)])

        def st_u(c):
            for h in range(2):
                xw = [("arena_free",)] if (c == NA - 1 and h == 1) else []
                ap, keys = in_mm(c, 2 * h, "u", xw)
                sl = slice(h * 1024, (h + 1) * 1024)

                def f(e, ap=ap, sl=sl):
                    return e.tensor_tensor(out=sz[:, sl], in0=sz[:, sl], in1=ap, op=ALU.mult)
                pg.op("dve", f, w=keys + [("sz", h)])

        def st_s(c):
            for h in range(2):
                ap, keys = ring.take(2)

                def f(e, ap=ap, h=h):
                    ins = None
                    for gl in range(4):
                        g = 4 * h + gl
                        ins = e.matmul(ap[:, gl * 256:(gl + 1) * 256], wsT[:, g, :], vn[:, g * 256:(g + 1) * 256],
                                       start=True, stop=True)
                    return ins
                pg.op("pe", f, r=[("wsT",), ("vn", h)], w=keys, name="s%d_%d" % (c, h))

                def fy(e, ap=ap, h=h):
                    ins = None
                    for gl in range(4):
                        g = 4 * h + gl
                        fs = slice(g * 256, (g + 1) * 256)
                        ins = e.scalar_tensor_tensor(out=y[:, fs], in0=ap[:, gl * 256:(gl + 1) * 256],
                                                     scalar=bspT[:, g:g + 1], in1=sz[:, fs],
                                                     op0=ALU.add, op1=ALU.mult)
                    return ins
                pg.op("dve", fy, r=[("bspT",), ("sz", h)], w=keys + [("y", h)])

        def st_Ty(c):
            ap, keys = ring.take(2)
            psb = ap.bitcast(BF16).rearrange("p (j q) -> p j q", q=P)

            def f(e):
                ins = None
                for fb in range(16):
                    ins = e.transpose(out=psb[:, fb, :], in_=y[:, fb * P:(fb + 1) * P], identity=ident[:, :])
                return ins
            pg.op("pe", f, r=[("y", 0), ("y", 1), ("ident",)], w=keys, name="Ty%d" % c)

            def fe(e):
                return e.activation(out=yT[:, :], in_=ap.bitcast(BF16), func=AF.Copy)
            pg.op("act", fe, w=keys + [("yT",)])

        def load_xr(c):
            s = c % 2

            def fl(e):
                return [e.dma_start(out=xr[s][:, :], in_=x_d[c * P:(c + 1) * P, :])]
            pg.op("sp", fl, w=[("xr", s)], dma="xr%d" % s)

        def st_o(c):
            ap, keys = ring.take(2)
            s = c % 2

            def f(e):
                ins = None
                for nh in range(2):
                    ins = mm_group(e, ap[:, nh * 512:(nh + 1) * 512],
                                   [(yT[:, fb * P:(fb + 1) * P], w_out[:, fb, nh * 512:(nh + 1) * 512])
                                    for fb in range(16)])
                return ins
            pg.op("pe", f, r=[("yT",), ("w_out", 0), ("w_out", 1)], w=keys, name="o%d" % c)

            def store(e):
                return [e.dma_start(out=h1_d[c * P:(c + 1) * P, :], in_=ob[:, :])]
            eps = ln_epilogue(pg, "a_", c, ap, keys, xr[s][:, :], ("xr", s), ob,
                              (rstats, rmv, rrstd, rnmr), Gp, Bp, neghalf, store, "h1st",
                              aff_eng=("dve" if c == NA - 1 else "pool"))
            eps[0]()
            if c + 2 < NA:
                load_xr(c + 2)
            for ep in eps[1:]:
                ep()

        load_xr(0)
        load_xr(1)
        st_Th(0)
        st_v(0)
        for c in range(NA):
            last = (c == NA - 1)
            if c > 0:
                st_Ty(c - 1)
            st_z(c)
            if c + 1 < NA:
                st_Th(c + 1)
            if c > 0:
                st_o(c - 1)
            st_u(c)
            if prefetchB and last:
                b_small_loads(pg, T, arena)
                b_weight_loads(pg, T, arena, True, [], [("arena_free",)])
            st_s(c)
            if prefetchB and c == 4:
                b_convert(pg, T)
            if c + 1 < NA:
                st_v(c + 1)
        st_Ty(NA - 1)
        st_o(NA - 1)

        pg.emit(nc, semstack)
    return [(pg.sems[("dma", "h1st")], pg.final[("dma", "h1st")])]


def phase_B(nc, T, ps, semstack, arena, pre_waits=(), preloaded=False):
    pg = Prog()
    ring = PsumRing(ps)
    with ExitStack() as es:
        def sb(name, shape, dtype):
            return es.enter_context(nc.sbuf_tensor(name, shape, dtype))

        wq, wk, wv, wz, wo, biasT, _flat = b_views(arena)
        if preloaded:
            _hb0, _hb1, Gp, Bp, es_t, flag, ident = b_small_views(arena)
            hb = [_hb0, _hb1]
        else:
            Gp = sb("b_Gp", [P, D], F32)
            Bp = sb("b_Bp", [P, D], F32)
            ident = sb("b_ident", [P, P], BF16)
            es_t = sb("b_es", [P, 16], F32)
            flag = sb("b_flag", [P, 1], F32)
            hb = [sb("b_hb%d" % i, [P, D], BF16) for i in range(2)]
        h1Ts = [sb("b_h1T%d" % i, [P, D], BF16) for i in range(2)]
        qT = sb("b_qT", [P, D], BF16)
        qtm = sb("b_qtm", [P, D], BF16)
        kT = [sb("b_kT%d" % i, [P, P], BF16) for i in range(3)]
        va = [sb("b_va%d" % i, [P, 2, 66], BF16) for i in range(3)]
        t1s = [sb("b_t1_%d" % i, [P, D], F32) for i in range(2)]
        tmp = [sb("b_tmp%d" % i, [P, 1024], F32) for i in range(2)]
        eTb = [[sb("b_eT%d_%d" % (i, b), [P, 16 * P], BF16) for b in range(2)] for i in range(2)]
        den = sb("b_den", [P, 16], F32)
        rec = sb("b_rec", [P, 16], F32)
        t2 = sb("b_t2", [P, 16, 64], F32)
        y2 = sb("b_y2", [P, D], BF16)
        y2T = sb("b_y2T", [P, D], BF16)
        hr = [sb("b_hr%d" % i, [P, D], F32) for i in range(2)]
        obs = [sb("b_ob%d" % i, [P, D], F32) for i in range(2)]
        stt2 = [(sb("b_rstats%d" % i, [P, 2, 6], F32), sb("b_rmv%d" % i, [P, 2], F32),
                 sb("b_rrstd%d" % i, [P, 1], F32), sb("b_rnmr%d" % i, [P, 1], F32)) for i in range(2)]
        neghalf = sb("b_neghalf", [P, 1], F32)

        def f_nh(e):
            return e.memset(neghalf[:, :], -0.5)
        pg.op("pool", f_nh, w=[("neghalf",)])

        h1_d = T["h1"]
        out_d = T["out"]

        def ld(dst, src, key, stream, eng="pool"):
            def f(e):
                return [e.dma_start(out=dst, in_=src)]
            pg.op(eng, f, w=[key], dma=stream)

        def load_hb(i):
            s = i % 2

            def f(e):
                return [e.dma_start(out=hb[s][:, :], in_=h1_d[i * P:(i + 1) * P, :])]
            pg.op("pool", f, w=[("hb", s)], dma="hb%d" % s)

        if not preloaded:
            ld(ident[:, :], T["ident"], ("ident",), "c_ident")
            load_hb(0)
            load_hb(1)
        if not preloaded:
            b_weight_loads(pg, T, arena, False, [])
        if not preloaded:
            ld(Gp[:, :], T["Gp1"], ("Gp",), "c_Gp", eng="sp")
            ld(Bp[:, :], T["Bp1"], ("Bp",), "c_Bp", eng="sp")
            ld(es_t[:, :], T["sinks"], ("es",), "c_es", eng="sp")
            ld(flag[:, :], T["flag"], ("flag",), "c_flag", eng="sp")

        def f_es(e):
            return e.activation(out=es_t[:, :], in_=es_t[:, :], func=AF.Exp)
        pg.op("act", f_es, w=[("es",)])

        def f_es2(e):
            return e.tensor_scalar(out=es_t[:, :], in0=es_t[:, :], scalar1=2.0, scalar2=None, op0=ALU.mult)
        pg.op("dve", f_es2, w=[("es",)])

        def f_ones(e):
            ins = None
            for i in range(3):
                ins = e.memset(va[i][:, :, 64:65], 2.0)
            return ins
        pg.op("dve", f_ones, w=[("va1", 0), ("va1", 1), ("va1", 2)])

        def f_flag1(e):
            return e.tensor_scalar(out=va[0][:, :, 64:65], in0=va[0][:, :, 64:65], scalar1=flag[:, 0:1],
                                   scalar2=None, op0=ALU.mult)
        pg.op("dve", f_flag1, r=[("flag",)], w=[("va1", 0)])

        def st_Th(i):
            s = i % 2
            ap, keys = ring.take(1)
            psb = ap.bitcast(BF16).rearrange("p (j q) -> p j q", q=P)

            def f(e):
                ins = None
                for kb in range(8):
                    ins = e.transpose(out=psb[:, kb, :], in_=hb[s][:, kb * P:(kb + 1) * P], identity=ident[:, :])
                return ins
            pg.op("pe", f, r=[("hb", s), ("ident",)], w=keys)

            def fe(e):
                return e.activation(out=h1Ts[s][:, :], in_=ap.bitcast(BF16)[:, 0:D], func=AF.Copy)
            pg.op("act", fe, w=keys + [("h1T", s)])
            if i + 2 < NA:
                load_hb(i + 2)

        def st_kv(i):
            h1T = h1Ts[i % 2]
            s3 = i % 3
            ap, keys = ring.take(1)
            if i == 3:
                def f_re(e):
                    return e.memset(va[0][:, :, 64:65], 2.0)
                pg.op("dve", f_re, w=[("va1", 0)])

            def f(e):
                mm_group(e, ap[:, 0:P], [(wk[:, kb, :], h1T[:, kb * P:(kb + 1) * P]) for kb in range(8)])
                return mm_group(e, ap[:, P:2 * P], [(h1T[:, kb * P:(kb + 1) * P], wv[:, kb, :]) for kb in range(8)])
            pg.op("pe", f, r=[("h1T", i % 2), ("wk",), ("wv",)], w=keys)

            def fk(e):
                return e.activation(out=kT[s3][:, :], in_=ap[:, 0:P], func=AF.Copy)
            pg.op("act", fk, w=keys + [("kT", s3)])

            if i == 0:
                def fv(e):
                    return e.tensor_scalar(out=va[s3][:, :, 0:64],
                                           in0=ap[:, P:2 * P].rearrange("p (a d) -> p a d", a=2),
                                           scalar1=flag[:, 0:1], scalar2=None, op0=ALU.mult)
                pg.op("dve", fv, r=[("flag",)], w=keys + [("va", s3)])
            else:
                def fv(e):
                    return e.activation(out=va[s3][:, :, 0:64],
                                        in_=ap[:, P:2 * P].rearrange("p (a d) -> p a d", a=2), func=AF.Copy)
                pg.op("act", fv, w=keys + [("va", s3)])

        def st_q(i):
            h1T = h1Ts[i % 2]
            ap, keys = ring.take(2)

            for nh in range(2):
                def f(e, nh=nh):
                    return mm_group(e, ap[:, nh * 512:(nh + 1) * 512],
                                    [(h1T[:, kb * P:(kb + 1) * P], wq[:, kb, nh * 512:(nh + 1) * 512])
                                     for kb in range(8)])
                pg.op("pe", f, r=[("h1T", i % 2), ("wq", nh)], w=[keys[nh]])

            def fe(e):
                return e.activation(out=qtm[:, :], in_=ap, func=AF.Copy, scale=0.125)
            pg.op("act", fe, w=keys + [("qtm",)])

        def st_qT(i):
            ap, keys = ring.take(1)
            psb = ap.bitcast(BF16).rearrange("p (j q) -> p j q", q=P)

            def f(e):
                ins = None
                for j in range(8):
                    ins = e.transpose(out=psb[:, j, :], in_=qtm[:, j * P:(j + 1) * P], identity=ident[:, :])
                return ins
            pg.op("pe", f, r=[("qtm",), ("ident",)], w=keys)

            def fe(e):
                return e.activation(out=qT[:, :], in_=ap.bitcast(BF16)[:, 0:D], func=AF.Copy)
            pg.op("act", fe, w=keys + [("qT",)])

        def st_z(i):
            h1T = h1Ts[i % 2]
            ap, keys = ring.take(2)
            t1 = t1s[i % 2]
            kt1 = ("t1", i % 2)

            def f(e):
                ins = None
                for nh in range(2):
                    ins = mm_group(e, ap[:, nh * 512:(nh + 1) * 512],
                                   [(h1T[:, kb * P:(kb + 1) * P], wz[:, kb, nh * 512:(nh + 1) * 512])
                                    for kb in range(8)])
                return ins
            pg.op("pe", f, r=[("h1T", i % 2), ("wz", 0), ("wz", 1)], w=keys)

            def ft(e):
                return e.activation(out=t1[:, :], in_=ap, func=AF.Tanh, scale=0.5)
            pg.op("act", ft, w=keys + [kt1])

            def f1(e):
                return e.scalar_tensor_tensor(out=t1[:, :], in0=t1[:, :], scalar=1.0, in1=ap,
                                              op0=ALU.add, op1=ALU.mult)
            pg.op("dve", f1, w=keys + [kt1])

        tmp_ctr = [0]

        def st_scores(i):
            es_ = i % 2
            for b in range(2):
                ks = (i - 1 + b) % 3
                for half in range(2):
                    ap, keys = ring.take(2)
                    h0 = half * 8
                    rows = slice(half * 64, (half + 1) * 64)

                    for bk in range(2):
                        def f(e, ap=ap, ks=ks, rows=rows, bk=bk):
                            return e.matmul(ap[:, bk * 512:(bk + 1) * 512], kT[ks][rows, :],
                                            qT[rows, bk * 512:(bk + 1) * 512], start=True, stop=True)
                        pg.op("pe", f, r=[("kT", ks), ("qT",)], w=[keys[bk]])
                    ts = tmp_ctr[0] % 2
                    tmp_ctr[0] += 1
                    if b == 0 and half == 0:
                        for bk in range(2):
                            def fb(e, ap=ap, b=b, h0=h0, ts=ts, bk=bk):
                                c0 = h0 * P + bk * 512
                                return e.tensor_tensor(out=tmp[ts][:, bk * 512:(bk + 1) * 512],
                                                       in0=ap[:, bk * 512:(bk + 1) * 512],
                                                       in1=biasT[:, b, c0:c0 + 512], op=ALU.add)
                            pg.op("dve", fb, r=[("biasT",)], w=[keys[bk], ("tmp", ts)])
                    else:
                        def fb(e, ap=ap, b=b, h0=h0, ts=ts):
                            return e.tensor_tensor(out=tmp[ts][:, :], in0=ap,
                                                   in1=biasT[:, b, h0 * P:(h0 + 8) * P], op=ALU.add)
                        pg.op("dve", fb, r=[("biasT",)], w=keys + [("tmp", ts)])

                    def fx(e, b=b, h0=h0, ts=ts, es_=es_):
                        return e.activation(out=eTb[es_][b][:, h0 * P:(h0 + 8) * P],
                                            in_=tmp[ts][:, :], func=AF.Exp)
                    pg.op("act", fx, r=[("tmp", ts)], w=[("eT", es_, b, half)])

        deferred = []

        def flush_y2():
            for fy, hg, i in deferred:
                pg.op("pool", fy, r=[("t2", hg), ("t1", i % 2)], w=[("y2", hg)])
            del deferred[:]

        def st_pv(i):
            es_ = i % 2
            for hg in range(2):
                ap, keys = ring.take(2)
                o3 = ap.rearrange("p (h q) -> p h q", q=P)

                def f(e, o3=o3, hg=hg):
                    ins = None
                    for hl in range(8):
                        h = hg * 8 + hl
                        for b in range(2):
                            ks = (i - 1 + b) % 3
                            ins = e.matmul(o3[:, hl, 0:65], eTb[es_][b][:, h * P:(h + 1) * P],
                                           va[ks][:, hg, 0:65], start=(b == 0), stop=(b == 1))
                    return ins
                rd = [("eT", es_, b, hg) for b in range(2)]
                rd += [("va", (i - 1) % 3), ("va", i % 3), ("va1", (i - 1) % 3), ("va1", i % 3)]
                pg.op("pe", f, r=rd, w=keys)

                def fd(e, o3=o3, hg=hg):
                    return e.tensor_tensor(out=den[:, hg * 8:(hg + 1) * 8], in0=o3[:, :, 64],
                                           in1=es_t[:, hg * 8:(hg + 1) * 8], op=ALU.add)
                pg.op("dve", fd, r=[("es",)], w=keys + [("den", hg)])

                def fr(e, hg=hg):
                    return e.reciprocal(out=rec[:, hg * 8:(hg + 1) * 8], in_=den[:, hg * 8:(hg + 1) * 8])
                pg.op("dve", fr, r=[("den", hg)], w=[("rec", hg)])

                def f2(e, o3=o3, hg=hg):
                    return e.tensor_tensor(out=t2[:, hg * 8:(hg + 1) * 8, :], in0=o3[:, :, 0:64],
                                           in1=rec[:, hg * 8:(hg + 1) * 8].unsqueeze(2).to_broadcast([P, 8, 64]),
                                           op=ALU.mult)
                pg.op("dve", f2, r=[("rec", hg)], w=keys + [("t2", hg)])

                def fy(e, hg=hg, t1=t1s[i % 2]):
                    sl = slice(hg * 512, (hg + 1) * 512)
                    return e.tensor_tensor(out=y2[:, sl],
                                           in0=t2[:, hg * 8:(hg + 1) * 8, :].rearrange("p h d -> p (h d)"),
                                           in1=t1[:, sl], op=ALU.mult)
                deferred.append((fy, hg, i))

        def st_Ty(i):
            ap, keys = ring.take(1)
            psb = ap.bitcast(BF16).rearrange("p (j q) -> p j q", q=P)

            def f(e):
                ins = None
                for fb in range(8):
                    ins = e.transpose(out=psb[:, fb, :], in_=y2[:, fb * P:(fb + 1) * P], identity=ident[:, :])
                return ins
            pg.op("pe", f, r=[("y2", 0), ("y2", 1), ("ident",)], w=keys)

            def fe(e):
                return e.activation(out=y2T[:, :], in_=ap.bitcast(BF16)[:, 0:D], func=AF.Copy)
            pg.op("act", fe, w=keys + [("y2T",)])
            s = i % 2

            def fl(e):
                return [e.dma_start(out=hr[s][:, :], in_=h1_d[i * P:(i + 1) * P, :])]
            pg.op("sp", fl, w=[("hr", s)], dma="hr%d" % s)

        def st_o(i):
            c = i - 1
            ap, keys = ring.take(2)
            s = i % 2

            def f(e):
                ins = None
                for nh in range(2):
                    ins = mm_group(e, ap[:, nh * 512:(nh + 1) * 512],
                                   [(y2T[:, fb * P:(fb + 1) * P], wo[:, fb, nh * 512:(nh + 1) * 512])
                                    for fb in range(8)])
                return ins
            pg.op("pe", f, r=[("y2T",), ("wo", 0), ("wo", 1)], w=keys)

            ob = obs[i % 2]

            def store(e):
                return [e.dma_start(out=out_d[c * P:(c + 1) * P, :], in_=ob[:, :])]
            return ln_epilogue(pg, "b%d_" % (i % 2), c, ap, keys, hr[s][:, :], ("hr", s), ob,
                               stt2[i % 2], Gp, Bp, neghalf, store, "outst", aff_eng="pool", split_r=True)

        st_Th(0)
        st_Th(1)
        st_kv(0)
        st_q(1)
        st_z(1)
        st_qT(1)
        st_kv(1)
        st_Th(2)
        st_scores(1)
        pending = []
        for j in range(1, NA):
            if j + 1 < NA:
                st_q(j + 1)
            flush_y2()
            if pending:
                pending[0]()
                pending[1]()
            if j + 1 < NA:
                st_z(j + 1)
            if j - 1 >= 1:
                st_Ty(j - 1)
            if j + 1 < NA:
                st_qT(j + 1)
            if pending:
                pending[2]()
            if j + 1 < NA:
                st_kv(j + 1)
            if j + 2 < NA:
                st_Th(j + 2)
            st_pv(j)
            late = None
            if pending:
                pending[3]()
                late = pending[4]
            pending = []
            if j - 1 >= 1:
                eps = st_o(j - 1)
                eps[0]()
                pending = list(eps[1:])
            if j + 1 < NA:
                st_scores(j + 1)
            if late:
                late()
        flush_y2()
        st_Ty(NA - 1)
        eps_last = st_o(NA - 1)
        for ep in pending:
            ep()
        for ep in eps_last:
            ep()

        pg.emit(nc, semstack, pre_waits)


def build(mode):
    nc = bass.Bass("TRN2", target_bir_lowering=False)
    T = {}

    def din(name, shape):
        T[name] = nc.dram_tensor(name, list(shape), F32, kind="ExternalInput").ap()

    doA = "A" in mode
    doB = "B" in mode
    din("ident", [P, P])
    if doA:
        din("x", [NA * P, D])
        din("w_in_a", [D, 3 * AW])
        din("w_out_a", [AW, D])
        din("Gv", [P, AW])
        din("Bv", [P, AW])
        din("Gp0", [P, D])
        din("Bp0", [P, D])
        din("wsT", [P, 8, P])
        din("trilT", [P, P])
        din("bspT", [P, 8])
    if doB:
        din("w_kv", [D, 2 * P])
        din("w_q", [D, D])
        din("w_z", [D, D])
        din("w_out_b", [D, D])
        din("biasT", [P, 2, 16 * P])
        din("Gp1", [P, D])
        din("Bp1", [P, D])
        din("sinks", [P, 16])
        din("flag", [P, 1])
        T["out"] = nc.dram_tensor("out", [NCH * P, D], F32, kind="ExternalOutput").ap()
    if doA and doB:
        T["h1"] = nc.dram_tensor("h1", [NA * P, D], F32, kind="Internal").ap()
    elif doA:
        T["h1"] = nc.dram_tensor("h1", [NA * P, D], F32, kind="ExternalOutput").ap()
    else:
        din("h1", [NA * P, D])

    if doA and doB:
        for _, dst in _BW:
            T[dst] = nc.dram_tensor(dst, [P, 8, D], BF16, kind="Internal").ap()
        T["wkv_bf"] = nc.dram_tensor("wkv_bf", [P, 2, 8, P], BF16, kind="Internal").ap()
    with ExitStack() as semstack, nc.psum_tensor("ps", [P, 4096], F32) as ps, \
            nc.sbuf_tensor("arena", [P, 8, 6144], BF16) as arena:
        pre = ()
        if doA:
            pre = phase_A(nc, T, ps, semstack, arena, prefetchB=doB)
        if doB:
            phase_B(nc, T, ps, semstack, arena, pre, preloaded=doA)
    return nc


def _rel_bucket_np(d):
    max_exact = 16
    df = np.maximum(d, 1).astype(np.float32)
    large = max_exact + (np.log(df / np.float32(max_exact)) / np.float32(np.log(128 / max_exact))
                         * np.float32(32 - max_exact)).astype(np.int32)
    large = np.minimum(large, 31)
    return np.where(d < max_exact, d, large)


def _bias_layout(rel_bias):
    j = np.arange(P)[:, None]
    t = np.arange(P)[None, :]
    out = np.empty((P, 2, 16, P), np.float32)
    for b in range(2):
        d = t - j + (P if b == 0 else 0)
        valid = (d >= 0) & (d < P)
        bk = _rel_bucket_np(np.clip(d, 0, P - 1))
        g = rel_bias[bk]
        g = np.transpose(g, (0, 2, 1))
        out[:, b] = np.where(valid[:, None, :], g, np.float32(NEG))
    return np.ascontiguousarray(out.reshape(P, 2, 16 * P))


def _bcast(v, n):
    return np.ascontiguousarray(np.broadcast_to(np.asarray(v, np.float32).reshape(1, n), (P, n)))


def _prep(inputs):
    f = lambda a: np.ascontiguousarray(np.asarray(a, dtype=np.float32))
    x = f(inputs["x"])
    shared = {"ident": np.eye(P, dtype=np.float32)}
    shared["w_in_a"] = f(inputs["w_in_a"][0])
    shared["w_out_a"] = f(inputs["w_out_a"][0])
    shared["Gv"] = _bcast(inputs["sgu_ln_g"][0], AW)
    shared["Bv"] = _bcast(inputs["sgu_ln_b"][0], AW)
    shared["Gp0"] = _bcast(inputs["post_ln_g"][0], D)
    shared["Bp0"] = _bcast(inputs["post_ln_b"][0], D)
    shared["Gp1"] = _bcast(inputs["post_ln_g"][1], D)
    shared["Bp1"] = _bcast(inputs["post_ln_b"][1], D)
    ws = f(inputs["w_spatial"][0])
    shared["wsT"] = np.ascontiguousarray(np.transpose(ws, (2, 0, 1)))
    shared["trilT"] = np.ascontiguousarray(np.triu(np.ones((P, P), np.float32)))
    shared["bspT"] = np.ascontiguousarray(f(inputs["b_spatial"][0]).T)
    shared["w_kv"] = f(inputs["w_kv"])
    wb = f(inputs["w_in_b"][0])
    wq = wb[:, :D].reshape(D, 2, 8, 64)
    shared["w_q"] = np.ascontiguousarray(np.transpose(wq, (0, 2, 1, 3)).reshape(D, D))
    shared["w_z"] = np.ascontiguousarray(wb[:, D:])
    shared["w_out_b"] = f(inputs["w_out_b"][0])
    shared["biasT"] = _bias_layout(f(inputs["rel_bias"]))
    shared["sinks"] = _bcast(inputs["attn_sinks"][0], 16)
    per_core = []
    half = NCH * P
    for core in range(N_CORES):
        bi, hi = core // 2, core % 2
        xc = np.zeros((NA * P, D), np.float32)
        xc[P:] = x[bi, hi * half:(hi + 1) * half]
        if hi == 1:
            xc[:P] = x[bi, half - P:half]
        flag = np.full((P, 1), 1.0 if hi == 1 else 0.0, np.float32)
        per_core.append({"x": xc, "flag": flag})
    return shared, per_core


_A_KEYS = ("ident", "x", "w_in_a", "w_out_a", "Gv", "Bv", "Gp0", "Bp0", "wsT", "trilT", "bspT")
_B_KEYS = ("ident", "w_kv", "w_q", "w_z", "w_out_b", "biasT", "Gp1", "Bp1", "sinks", "flag")

FUSED = True
_NC_CACHE = {}


def _get_nc(mode):
    if mode not in _NC_CACHE:
        _NC_CACHE[mode] = build(mode)
    return _NC_CACHE[mode]


def kernel(**inputs):
    shared, per_core = _prep(inputs)
    cores = list(range(N_CORES))
    if FUSED:
        keys = tuple(dict.fromkeys(_A_KEYS + _B_KEYS))
        in_maps = []
        for c in cores:
            m = {k: (shared[k] if k in shared else per_core[c][k]) for k in keys}
            in_maps.append(m)
        res = run_bass_kernel_spmd(_get_nc("AB"), in_maps, core_ids=cores)
        outs = [np.asarray(r["out"]) for r in res.results]
    else:
        in_maps = [{k: (shared[k] if k in shared else per_core[c][k]) for k in _A_KEYS} for c in cores]
        resA = run_bass_kernel_spmd(_get_nc("A"), in_maps, core_ids=cores)
        in_maps = []
        for c in cores:
            m = {k: (shared[k] if k in shared else per_core[c][k]) for k in _B_KEYS}
            m["h1"] = np.asarray(resA.results[c]["h1"])
            in_maps.append(m)
        resB = run_bass_kernel_spmd(_get_nc("B"), in_maps, core_ids=cores)
        outs = [np.asarray(r["out"]) for r in resB.results]
    out = np.empty((4, 2 * NCH * P, D), np.float32)
    for c in cores:
        out[c // 2, (c % 2) * NCH * P:((c % 2) + 1) * NCH * P] = outs[c].reshape(NCH * P, D)
    return out
```
